# Optimizing a Trainium2 kernel written in Bass

```python
import jax, jax.numpy as jnp
from jax import lax
import numpy as np

D_MODEL = 1024
BATCH = 8
SEQ = 4096
DEPTH = 1

CHUNK = 64
CONV_WIDTH = D_MODEL // 2
CONV_KERNEL = 31
HG_WIDTH = D_MODEL // 2
HG_HEAD_DIM = 128
HG_HEADS = HG_WIDTH // HG_HEAD_DIM
D_FF = 4 * D_MODEL
N_BRANCHES = 2
NORM_EPS = 1e-6
LN_EPS = 1e-5
IN_COLS = 2 * CONV_WIDTH + 4 * HG_WIDTH + N_BRANCHES * D_MODEL

kernel_name = "conv_hgrn2_gated_hybrid_block"


def rms_norm(x, w):
    xf = x.astype(jnp.float32)
    y = xf * lax.rsqrt(jnp.mean(xf * xf, axis=-1, keepdims=True) + NORM_EPS)
    return (y * w.astype(jnp.float32)).astype(x.dtype)


def layer_norm(x, w, b):
    xf = x.astype(jnp.float32)
    mu = jnp.mean(xf, axis=-1, keepdims=True)
    xc = xf - mu
    var = jnp.mean(xc * xc, axis=-1, keepdims=True)
    y = xc * lax.rsqrt(var + LN_EPS) * w.astype(jnp.float32) + b.astype(jnp.float32)
    return y.astype(x.dtype)


def split_points():
    sizes = [CONV_WIDTH, CONV_WIDTH, HG_WIDTH, HG_WIDTH, HG_WIDTH, HG_WIDTH, D_MODEL]
    pts, acc = [], 0
    for s in sizes:
        acc += s
        pts.append(acc)
    return pts


def conformer_conv(a, a_gate, dw_w, dw_b, ln_w, ln_b, w_pw, b_pw):
    u = a * jax.nn.sigmoid(a_gate)
    u = lax.conv_general_dilated(
        u, dw_w.astype(u.dtype), window_strides=(1,),
        padding=[(CONV_KERNEL - 1, 0)],
        dimension_numbers=("NWC", "WIO", "NWC"),
        feature_group_count=CONV_WIDTH) + dw_b
    u = jax.nn.silu(layer_norm(u, ln_w, ln_b))
    return u @ w_pw + b_pw


def hgrn2_step(state, inp):
    q, k, v, g = inp
    b = jnp.cumsum(g, axis=2)
    o_inter = jnp.einsum("bhtk,bhkv->bhtv", q * jnp.exp(b), state)
    causal = jnp.tril(jnp.ones((CHUNK, CHUNK), dtype=bool))
    diff = b[:, :, :, None, :] - b[:, :, None, :, :]
    decay = jnp.exp(jnp.where(causal[None, None, :, :, None], diff, -jnp.inf))
    scores = jnp.einsum("bhtk,bhsk,bhtsk->bhts", q, k, decay)
    o = o_inter + jnp.einsum("bhts,bhsv->bhtv", scores, v)
    b_last = b[:, :, -1:, :]
    new_state = jnp.exp(b_last[:, :, 0, :])[..., None] * state + jnp.einsum(
        "bhsk,bhsv->bhkv", k * jnp.exp(b_last - b), v)
    return new_state, o


def hgrn2(q, f_logit, i, g_out, lb, norm_w, w_o):
    B, S, _ = q.shape
    n_chunks = S // CHUNK
    f = lb + (1.0 - lb) * jax.nn.sigmoid(f_logit.astype(jnp.float32))
    log_f = jnp.log(f)
    k = 1.0 - f

    def to_chunks(t):
        return t.astype(jnp.float32).reshape(B, n_chunks, CHUNK, HG_HEADS, HG_HEAD_DIM).transpose(1, 0, 3, 2, 4)

    qs, ks, vs, gs = to_chunks(q), to_chunks(k), to_chunks(i), to_chunks(log_f)
    s0 = jnp.zeros((B, HG_HEADS, HG_HEAD_DIM, HG_HEAD_DIM), jnp.float32)
    _, o = lax.scan(hgrn2_step, s0, (qs, ks, vs, gs))
    o = o.transpose(1, 0, 3, 2, 4).reshape(B, S, HG_HEADS, HG_HEAD_DIM)
    o = o * lax.rsqrt(jnp.mean(o * o, axis=-1, keepdims=True) + NORM_EPS)
    o = o * norm_w.astype(jnp.float32).reshape(HG_HEADS, HG_HEAD_DIM)
    o = o.reshape(B, S, HG_WIDTH) * jax.nn.silu(g_out.astype(jnp.float32))
    return o.astype(q.dtype) @ w_o


def setup_inputs(seed: int = 0) -> dict:
    key = jax.random.key(seed)
    ks = jax.random.split(key, 17)

    def nrm(k, shape, scale):
        return jax.random.normal(k, shape, jnp.float32) * scale

    return {
        "x": nrm(ks[0], (BATCH, SEQ, D_MODEL), 1.0),
        "norm_mix_w": 1.0 + nrm(ks[1], (DEPTH, D_MODEL), 0.02),
        "w_in": nrm(ks[2], (DEPTH, D_MODEL, IN_COLS), D_MODEL ** -0.5),
        "dw_conv_w": nrm(ks[3], (DEPTH, CONV_KERNEL, 1, CONV_WIDTH), CONV_KERNEL ** -0.5),
        "dw_conv_b": nrm(ks[4], (DEPTH, CONV_WIDTH), 0.02),
        "conv_ln_w": 1.0 + nrm(ks[5], (DEPTH, CONV_WIDTH), 0.02),
        "conv_ln_b": nrm(ks[6], (DEPTH, CONV_WIDTH), 0.02),
        "w_conv_out": nrm(ks[7], (DEPTH, CONV_WIDTH, D_MODEL), CONV_WIDTH ** -0.5),
        "b_conv_out": nrm(ks[8], (DEPTH, D_MODEL), 0.02),
        "hgrn_lb": nrm(ks[9], (DEPTH + 1, HG_WIDTH), 0.1),
        "hgrn_norm_w": 1.0 + nrm(ks[10], (DEPTH, HG_WIDTH), 0.02),
        "w_hgrn_out": nrm(ks[11], (DEPTH, HG_WIDTH, D_MODEL), HG_WIDTH ** -0.5),
        "w_out": nrm(ks[12], (DEPTH, D_MODEL, D_MODEL), D_MODEL ** -0.5),
        "norm_mlp_w": 1.0 + nrm(ks[13], (DEPTH, D_MODEL), 0.02),
        "w_mlp_up": nrm(ks[14], (DEPTH, D_MODEL, D_FF), D_MODEL ** -0.5),
        "w_mlp_down": nrm(ks[15], (DEPTH, D_FF, D_MODEL), D_FF ** -0.5),
        "norm_final_w": 1.0 + nrm(ks[16], (D_MODEL,), 0.02),
    }


def reference(x, norm_mix_w, w_in, dw_conv_w, dw_conv_b, conv_ln_w, conv_ln_b,
              w_conv_out, b_conv_out, hgrn_lb, hgrn_norm_w, w_hgrn_out, w_out,
              norm_mlp_w, w_mlp_up, w_mlp_down, norm_final_w):
    lower_bounds = jnp.cumsum(jax.nn.softmax(hgrn_lb.astype(jnp.float32), axis=0), axis=0)
    pts = split_points()
    for l in range(DEPTH):
        h = rms_norm(x, norm_mix_w[l])
        proj = h @ w_in[l]
        a, a_gate, hq, hf, hi, hg, gate_a, gate_b = jnp.split(proj, pts, axis=-1)
        y_conv = conformer_conv(a, a_gate, dw_conv_w[l], dw_conv_b[l], conv_ln_w[l], conv_ln_b[l],
                                w_conv_out[l], b_conv_out[l])
        y_rec = hgrn2(hq, hf, hi, hg, lower_bounds[l], hgrn_norm_w[l], w_hgrn_out[l])
        y = jax.nn.sigmoid(gate_a) * y_conv + jax.nn.sigmoid(gate_b) * y_rec
        x = x + y @ w_out[l]
        h = rms_norm(x, norm_mlp_w[l])
        x = x + jnp.square(jax.nn.relu(h @ w_mlp_up[l])) @ w_mlp_down[l]
    return rms_norm(x, norm_final_w)
```

```python
from contextlib import ExitStack
import numpy as np
import ml_dtypes
import concourse.bass as bass
import concourse.mybir as mybir
from concourse.bass_utils import run_bass_kernel_spmd

F32 = mybir.dt.float32
BF16 = mybir.dt.bfloat16
AF = mybir.ActivationFunctionType
ALU = mybir.AluOpType

P = 128
D = 1024
SEQ = 4096
NCORES = 8
TB = 256
NT = TB // P
KW = 31
NH = 4
CH = 64
DFF = 4096
INC = 5120
OA, OAG, OHQ, OHF, OHI, OHG, OGA, OGB = 0, 512, 1024, 1536, 2048, 2560, 3072, 4096
KC = D // P
C_DW, C_DWB, C_LNW, C_LNB, C_BCO, C_LB0, C_LB1, C_HNW, C_NW1, C_NW2, C_EPSN, C_EPSL = \
    0, 124, 128, 132, 136, 144, 148, 152, 156, 164, 172, 173
NPAR = 176


class Buf:
    __slots__ = ("name", "w", "r")

    def __init__(self, name, r=None):
        self.name = name
        self.w = None
        self.r = list(r) if r else []


class Tl:
    __slots__ = ("ap", "b")

    def __init__(self, ap, b):
        self.ap = ap
        self.b = b

    def __getitem__(self, k):
        return self.ap[k]


class Rot:
    def __init__(self, tiles):
        self.t = tiles
        self.i = 0

    def next(self):
        t = self.t[self.i % len(self.t)]
        self.i += 1
        return t


class Sched:
    ENGS = ("pe", "act", "dve", "pool", "sp")

    def __init__(self, nc, es, n_dma_sems=24):
        self.nc = nc
        self.ops = {e: [] for e in self.ENGS}
        self.sems = {}
        self.cnt = {}
        for e in self.ENGS:
            self.sems[e] = es.enter_context(nc.semaphore("s_" + e))
            self.cnt[e] = 0
        self.dma_sems = {"sp": [], "pool": []}
        for q, nq in (("sp", n_dma_sems), ("pool", 16)):
            for i in range(nq):
                k = "d%s%d" % (q, i)
                self.sems[k] = es.enter_context(nc.semaphore("s_" + k))
                self.cnt[k] = 0
                self.dma_sems[q].append(k)
        self.dma_rr = {"sp": 0, "pool": 0}
        self.seen = {e: {} for e in self.ENGS}
        self.nb = 0
        self.targets = {e: set() for e in self.ENGS}
        self.lastc = {e: 0 for e in self.ENGS}

    def buf(self, name=None, fence=None):
        self.nb += 1
        return Buf(name or "b%d" % self.nb, fence)

    def fence_all(self):
        f = [(e, self.lastc[e]) for e in self.ENGS if self.lastc[e] > 0]
        return f + [(k, self.cnt[k]) for q in ("sp", "pool") for k in self.dma_sems[q] if self.cnt[k] > 0]

    def _deps(self, eng, reads, writes):
        need = {}

        def add(tok):
            k, v = tok
            if need.get(k, 0) < v:
                need[k] = v
        for b in reads:
            if b.w is not None:
                add(b.w)
        for b in writes:
            if b.w is not None:
                add(b.w)
            for t in b.r:
                add(t)
        waits = []
        for k, v in need.items():
            if k == "pe" and eng == "pe":
                continue
            if self.seen[eng].get(k, 0) >= v:
                continue
            self.seen[eng][k] = v
            waits.append((k, v))
            if k in self.targets:
                self.targets[k].add(v)
        return waits

    def _commit(self, tok, reads, writes):
        for b in writes:
            b.w = tok
            b.r = []
        for b in reads:
            if b in writes:
                continue
            b.r.append(tok)
            if len(b.r) > 16:
                m = {}
                for k, v in b.r:
                    if m.get(k, 0) < v:
                        m[k] = v
                b.r = list(m.items())

    def op(self, eng, fn, reads=(), writes=()):
        waits = self._deps(eng, reads, writes)
        self.cnt[eng] += 1
        self.lastc[eng] = self.cnt[eng]
        tok = (eng, self.cnt[eng])
        self.ops[eng].append((waits, fn, self.cnt[eng], None))
        self._commit(tok, reads, writes)
        return tok

    def dma(self, eng, fn, reads=(), writes=()):
        waits = self._deps(eng, reads, writes)
        k = self.dma_sems[eng][self.dma_rr[eng] % len(self.dma_sems[eng])]
        self.dma_rr[eng] += 1
        prev = self.cnt[k]
        if prev > 0 and self.seen[eng].get(k, 0) < prev:
            self.seen[eng][k] = prev
            waits.append((k, prev))
        self.cnt[k] += 16
        tok = (k, self.cnt[k])
        self.cnt[eng] += 1
        self.ops[eng].append((waits, fn, self.cnt[eng], k))
        self._commit(tok, reads, writes)
        return tok

    def run(self):
        nc = self.nc
        sems = self.sems
        val = {}
        for e in self.ENGS:
            val[e] = {idx: i + 1 for i, idx in enumerate(sorted(self.targets[e]))}
        final = [(k, self.cnt[k]) for q in ("sp", "pool") for k in self.dma_sems[q] if self.cnt[k] > 0]
        ops, targets = self.ops, self.targets
        print("ops per engine", {e: len(ops[e]) for e in self.ENGS}, "sem incs", {e: len(targets[e]) for e in self.ENGS})

        def play(e, h):
            for waits, fn, idx, dsem in ops[e]:
                for k, v in waits:
                    h.wait_ge(sems[k], val[k][v] if k in val else v)
                ins = fn(h)
                if dsem is not None:
                    ins.then_inc(sems[dsem], 16)
                elif idx in targets[e]:
                    ins.then_inc(sems[e], 1)
            if e == "sp":
                for k, v in final:
                    h.wait_ge(sems[k], v)

        with nc.Block() as block:
            @block.tensor
            def _(h):
                play("pe", h)

            @block.scalar
            def _(h):
                play("act", h)

            @block.vector
            def _(h):
                play("dve", h)

            @block.gpsimd
            def _(h):
                play("pool", h)

            @block.sync
            def _(h):
                play("sp", h)


def build(T):
    NB = T // TB
    nc = bass.Bass("TRN2", target_bir_lowering=False)
    dram_in = lambda name, shape, dt: nc.dram_tensor(name, shape, dt, kind="ExternalInput").ap()
    x_d = dram_in("x", [T, D], F32)
    w_in_d = dram_in("w_in", [D, INC], F32)
    w_co_d = dram_in("w_conv_out", [512, D], F32)
    w_ho_d = dram_in("w_hgrn_out", [512, D], F32)
    w_out_d = dram_in("w_out", [D, D], F32)
    w_up_d = dram_in("w_mlp_up", [D, DFF], F32)
    w_dn_d = dram_in("w_mlp_down", [DFF, D], F32)
    par_d = dram_in("params", [P, NPAR], F32)
    nfw_d = dram_in("nfw", [P, D], F32)
    wd_d = dram_in("wd", [P, 16 * 8 * 32], F32)
    ident_d = dram_in("ident", [P, P], BF16)
    ones_d = dram_in("ones2", [P, 2 * P], BF16)
    mask_d = dram_in("mask", [P, P], BF16)
    y_d = nc.dram_tensor("y", [T, D], F32, kind="ExternalOutput").ap()
    x1_d = nc.dram_tensor("x1_scratch", [T, D], F32).ap()

    with ExitStack() as es:
        S = Sched(nc, es)
        ARENA_BYTES = 205 * 1024 + 512
        arena = es.enter_context(nc.sbuf_tensor("arena", [P, ARENA_BYTES // 2], BF16))

        OFFS = {}
        PITCH = ARENA_BYTES // 2

        class Bump:
            def __init__(self, start=0, fence=None):
                self.off = start
                self.fence = fence
                self.hi = start

            def alloc(self, name, free_shape, dt):
                n = int(np.prod(free_shape))
                nbytes = n * (4 if dt == F32 else 2)
                self.off = (self.off + 3) // 4 * 4
                a = arena[:, self.off // 2:(self.off + nbytes) // 2]
                if dt == F32:
                    a = a.bitcast(F32)
                if len(free_shape) == 2:
                    a = a.rearrange("p (a b) -> p a b", a=free_shape[0])
                elif len(free_shape) == 3:
                    a = a.rearrange("p (a b c) -> p a b c", a=free_shape[0], b=free_shape[1])
                OFFS[name] = self.off // 2
                self.off += nbytes
                self.hi = max(self.hi, self.off)
                assert self.off <= ARENA_BYTES, (name, self.off)
                return Tl(a, S.buf(name, self.fence))

        def small(name, shape, dt):
            t = es.enter_context(nc.sbuf_tensor(name, shape, dt))
            return Tl(t, S.buf(name))

        par = small("par", [P, NPAR], F32)
        ident = small("identb", [P, P], BF16)
        ones2 = small("ones2b", [P, 2 * P], BF16)
        maskt = small("maskt", [P, P], BF16)
        zeros = small("zeros", [P, CH], F32)
        lbA = small("lbA", [P, 12], F32)
        cols = small("cols", [P, 8], F32)
        colsF = small("colsF", [P, 8], F32)
        dcol = small("dcol", [P, 16], F32)
        cexp = small("cexp", [P, 2], F32)
        pcol = lambda c: par[:, c:c + 1]

        S.dma("sp", lambda h: h.dma_start(out=par[:], in_=par_d), writes=[par.b])
        S.dma("sp", lambda h: h.dma_start(out=ident[:], in_=ident_d), writes=[ident.b])
        S.dma("sp", lambda h: h.dma_start(out=ones2[:], in_=ones_d), writes=[ones2.b])
        S.dma("sp", lambda h: h.dma_start(out=maskt[:], in_=mask_d), writes=[maskt.b])
        S.op("dve", lambda h: h.memset(zeros[:], 0.0), writes=[zeros.b])
        S.op("dve", lambda h: h.memset(cexp[:, 0:1], -1.0), writes=[cexp.b])
        S.op("dve", lambda h: h.memset(cexp[:, 1:2], -0.5), writes=[cexp.b])
        S.op("dve", lambda h: h.tensor_tensor(out=lbA[:, 0:4], in0=par[:, C_LB0:C_LB0 + 4], in1=par[:, C_LB1:C_LB1 + 4],
                                              op=ALU.subtract), reads=[par.b], writes=[lbA.b])
        S.op("act", lambda h: h.activation(out=lbA[:, 0:4], in_=lbA[:, 0:4], func=AF.Sigmoid), reads=[lbA.b], writes=[lbA.b])
        S.op("dve", lambda h: h.tensor_scalar(out=lbA[:, 4:8], in0=lbA[:, 0:4], scalar1=-1.0, scalar2=1.0,
                                              op0=ALU.mult, op1=ALU.add), reads=[lbA.b], writes=[lbA.b])
        S.op("dve", lambda h: h.tensor_scalar(out=lbA[:, 8:12], in0=lbA[:, 0:4], scalar1=1.0, scalar2=-1.0,
                                              op0=ALU.mult, op1=ALU.add), reads=[lbA.b], writes=[lbA.b])

        psb = [es.enter_context(nc.psum_tensor("ps%d" % i, [P, 512], F32)) for i in range(8)]
        PJ = Rot([Tl(psb[i], S.buf("pj%d" % i)) for i in range(3)])
        CV = Rot([Tl(psb[i], S.buf("cv%d" % i)) for i in (3, 4)])
        OT = [Tl(psb[5], S.buf("ot0")), Tl(psb[6], S.buf("ot1"))]
        PSL = Tl(psb[7], S.buf("psl"))
        PB = Rot(PJ.t + CV.t)

        def mm(out_t, out_ap, lhsT, rhs, reads, start, stop, **kw):
            S.op("pe", lambda h: h.matmul(out=out_ap, lhsT=lhsT, rhs=rhs, start=start, stop=stop, **kw),
                 reads=reads, writes=[out_t.b])

        def split2(bank):
            return bank[:].rearrange("p (a t) -> p a t", a=2)

        A = Bump(0)
        Win = A.alloc("Win", [KC, INC], BF16)
        WinB = [S.buf("win%d" % g) for g in range(INC // 512)]
        Wco = A.alloc("Wco", [4, D], BF16)
        Who = A.alloc("Who", [4, D], BF16)
        Wout = A.alloc("Wout", [KC, D], BF16)
        WoutB = [S.buf("wout%d" % g) for g in range(2)]
        Wdt = A.alloc("Wdt", [16, 8, 32], BF16)
        w_in_v = w_in_d.rearrange("(kc p) c -> p kc c", p=P)
        for g in (OHF // 512, OAG // 512, OA // 512):
            S.dma("pool", lambda h, g=g: h.dma_start(out=Win[:, :, g * 512:(g + 1) * 512], in_=w_in_v[:, :, g * 512:(g + 1) * 512]),
                  writes=[WinB[g]])
        S.dma("pool", lambda h: h.dma_start(out=Wdt[:], in_=wd_d.rearrange("p (q g c) -> p q g c", q=16, g=8)), writes=[Wdt.b])
        for g in (OHQ // 512, OHI // 512, OHG // 512, 6, 7, 8, 9):
            S.dma("pool", lambda h, g=g: h.dma_start(out=Win[:, :, g * 512:(g + 1) * 512], in_=w_in_v[:, :, g * 512:(g + 1) * 512]),
                  writes=[WinB[g]])
        S.dma("pool", lambda h: h.dma_start(out=Wco[:], in_=w_co_d.rearrange("(kc p) c -> p kc c", p=P)), writes=[Wco.b])
        S.dma("pool", lambda h: h.dma_start(out=Who[:], in_=w_ho_d.rearrange("(kc p) c -> p kc c", p=P)), writes=[Who.b])
        w_out_v = w_out_d.rearrange("(kc p) c -> p kc c", p=P)
        for g in range(2):
            S.dma("pool", lambda h, g=g: h.dma_start(out=Wout[:, :, g * 512:(g + 1) * 512], in_=w_out_v[:, :, g * 512:(g + 1) * 512]),
                  writes=[WoutB[g]])

        UW = TB + 31
        RW = TB + 28
        xin = [A.alloc("xin%d" % j, [D], F32) for j in range(NT)]
        hT = A.alloc("hT", [KC, TB], BF16)
        U = A.alloc("U", [4, UW], BF16)
        R = A.alloc("R", [16, RW], BF16)
        Rb = [S.buf("R%d" % q) for q in range(16)]
        y32 = A.alloc("y32", [4, TB], F32)
        ysq = A.alloc("ysq", [4, TB], BF16)
        lnm = A.alloc("lnm", [TB], F32)
        lnv = A.alloc("lnv", [TB], F32)
        lnr = A.alloc("lnr", [TB], F32)
        zt = A.alloc("zt", [2, TB], F32)
        sgz = A.alloc("sgz", [2, TB], F32)
        vt = A.alloc("vt", [4, TB], BF16)
        T256 = Rot([A.alloc("t256_%d" % i, [TB], F32) for i in range(4)])
        EB = [A.alloc("eb%d" % i, [TB], F32) for i in range(NH)]
        X = [A.alloc("x512_%d" % i, [2, TB], F32) for i in range(7)]
        B512 = Rot([A.alloc("b512_%d" % i, [2, TB], BF16) for i in range(1)])
        qb = A.alloc("qb", [NH, TB], BF16)
        kbn = A.alloc("kbn", [NH, TB], BF16)
        kdn = [A.alloc("kdn%d" % i, [TB], BF16) for i in range(NH)]
        kdT = A.alloc("kdT", [NH * NT, P], BF16)
        vn = [A.alloc("vn%d" % j, [512], BF16) for j in range(NT)]
        sc = A.alloc("sc", [NH * NT, P], BF16)
        Sst = [[A.alloc("S%d_%d" % (h, i), [P], F32) for i in range(2)] for h in range(NH)]
        Sbf = [[A.alloc("Sb%d_%d" % (h, i), [P], BF16) for i in range(2)] for h in range(NH)]
        orn = A.alloc("orn", [NH, TB], BF16)
        yT = A.alloc("yT", [KC, TB], BF16)
        ybf = Tl(vt.ap, vt.b)
        xn = [Tl(sc.ap.rearrange("p a t -> p (a t)"), sc.b), Tl(kdT.ap.rearrange("p a t -> p (a t)"), kdT.b)]
        print("phase A arena bytes", A.hi, "of", ARENA_BYTES)

        for h_ in range(NH):
            S.op("dve", lambda h, t=Sst[h_][0]: h.memset(t[:], 0.0), writes=[Sst[h_][0].b])
            S.op("dve", lambda h, t=Sbf[h_][0]: h.memset(t[:], 0.0), writes=[Sbf[h_][0].b])
        S.op("dve", lambda h: h.memset(U[:], 0.0), writes=[U.b])
        scur = [0]
        x1B = [S.buf("x1d%d" % i) for i in range(NB * NT)]

        def proj_fm(bank, half, col0):
            for kc in range(KC):
                mm(bank, bank[:, half * TB:(half + 1) * TB], Win[:, kc, col0:col0 + P], hT[:, kc, :],
                   [WinB[col0 // 512], hT.b], kc == 0, kc == KC - 1)

        def stage_a(n):
            t0 = n * TB
            for j in range(NT):
                S.dma("sp", lambda h, j=j: h.dma_start(out=xin[j][:], in_=x_d[t0 + j * P:t0 + (j + 1) * P, :]), writes=[xin[j].b])
            S.op("dve", lambda h: h.memset(cols[:, 0:NT], 0.0), writes=[cols.b])
            for j in range(NT):
                S.op("act", lambda h, j=j: h.activation(out=xn[j][:], in_=xin[j][:], func=AF.Square, accum_out=cols[:, j:j + 1]),
                     reads=[xin[j].b, cols.b], writes=[xn[j].b, cols.b])
            S.op("dve", lambda h: h.tensor_scalar(out=cols[:, 2:2 + NT], in0=cols[:, 0:NT], scalar1=1.0 / D, scalar2=pcol(C_EPSN), op0=ALU.mult, op1=ALU.add),
                 reads=[cols.b, par.b], writes=[cols.b])
            S.op("pool", lambda h: h.tensor_tensor(out=cols[:, 4:4 + NT], in0=cols[:, 2:2 + NT], in1=cexp[:, 1:2].to_broadcast([P, NT]), op=ALU.pow),
                 reads=[cols.b, cexp.b], writes=[cols.b])
            for j in range(NT):
                S.op("act", lambda h, j=j: h.activation(out=xn[j][:], in_=xin[j][:], func=AF.Copy, scale=cols[:, 4 + j:5 + j]),
                     reads=[xin[j].b, cols.b], writes=[xn[j].b])

        def stage_b(n):
            for j in range(NT):
                bank = PJ.next()
                bv = bank[:].bitcast(BF16)
                for kc in range(KC):
                    S.op("pe", lambda h, kc=kc, bv=bv, j=j: h.transpose(out=bv[:, kc * P:(kc + 1) * P], in_=xn[j][:, kc * P:(kc + 1) * P], identity=ident[:]),
                         reads=[xn[j].b, ident.b], writes=[bank.b])
                S.op("dve", lambda h, j=j, bv=bv: h.tensor_tensor(
                    out=hT[:, :, j * P:(j + 1) * P], in0=bv.rearrange("p (k t) -> p k t", k=KC),
                    in1=par[:, C_NW1:C_NW1 + KC].unsqueeze(2).to_broadcast([P, KC, P]), op=ALU.mult),
                    reads=[bank.b, par.b], writes=[hT.b])

        def conv_chunk(c, cvbank):
            hh = c % 2
            if hh == 0:
                cvbank[0] = CV.next()
            bank = cvbank[0]
            for g in range(8):
                for q4 in range(4):
                    q = 4 * c + q4
                    mm(bank, bank[32 * q4:32 * q4 + 32, hh * TB:(hh + 1) * TB], Wdt[:, q, g, :], R[:, q, 4 * g:4 * g + TB],
                       [Wdt.b, Rb[q]], g == 0, g == 7, tile_position=(0, 32 * q4))
            if hh == 1:
                for h2 in range(2):
                    c2 = c - 1 + h2
                    S.op("act", lambda h, bank=bank, c2=c2, h2=h2: h.activation(
                        out=y32[:, c2, :], in_=bank[:, h2 * TB:(h2 + 1) * TB], func=AF.Identity, bias=pcol(C_DWB + c2)),
                        reads=[bank.b, par.b], writes=[y32.b])
                    S.op("act", lambda h, bank=bank, c2=c2, h2=h2: h.activation(
                        out=ysq[:, c2, :], in_=bank[:, h2 * TB:(h2 + 1) * TB], func=AF.Square, bias=pcol(C_DWB + c2)),
                        reads=[bank.b, par.b], writes=[ysq.b])
                    S.op("act", lambda h, bank=bank, c2=c2, h2=h2: h.activation(
                        out=ybf[:, c2, :], in_=bank[:, h2 * TB:(h2 + 1) * TB], func=AF.Identity, bias=pcol(C_DWB + c2)),
                        reads=[bank.b, par.b], writes=[ybf.b])

        def gate_proj(col0, dst):
            bank = PJ.next()
            for hh in range(2):
                proj_fm(bank, hh, col0 + hh * P)
            S.op("act", lambda h, bank=bank, dst=dst: h.activation(out=dst[:], in_=split2(bank), func=AF.Sigmoid),
                 reads=[bank.b], writes=[dst.b])
            return bank

        def blockA(n):
            hfb = [X[0], X[1]]
            for hp in range(2):
                gate_proj(OHF + 2 * hp * P, hfb[hp])
            def head_prep(HEADS):
                ebs = []
                for hd in HEADS:
                    hp, hh = hd // 2, hd % 2
                    sgf = hfb[hp]
                    Ft = T256.next()
                    S.op("act", lambda h, Ft=Ft, sgf=sgf, hh=hh, hd=hd: h.activation(
                        out=Ft[:], in_=sgf[:, hh, :], func=AF.Identity, scale=lbA[:, 4 + hd:5 + hd], bias=lbA[:, hd:hd + 1]),
                        reads=[sgf.b, lbA.b], writes=[Ft.b])
                    fm1 = T256.next()
                    S.op("act", lambda h, fm1=fm1, sgf=sgf, hh=hh, hd=hd: h.activation(
                        out=fm1[:], in_=sgf[:, hh, :], func=AF.Identity, scale=lbA[:, 4 + hd:5 + hd], bias=lbA[:, 8 + hd:9 + hd]),
                        reads=[sgf.b, lbA.b], writes=[fm1.b])
                    eb = EB[hd]
                    for c in range(TB // CH):
                        S.op("dve", lambda h, eb=eb, Ft=Ft, c=c: h.tensor_tensor_scan(
                            out=eb[:, c * CH:(c + 1) * CH], data0=Ft[:, c * CH:(c + 1) * CH], data1=zeros[:], initial=1.0,
                            op0=ALU.mult, op1=ALU.add), reads=[Ft.b, zeros.b], writes=[eb.b])
                    eb3 = eb[:].rearrange("p (c j) -> p c j", j=CH)
                    S.op("pool", lambda h, eb3=eb3, hd=hd: h.tensor_copy(out=dcol[:, hd * 4:hd * 4 + 4].unsqueeze(2), in_=eb3[:, :, CH - 1:CH]),
                         reads=[eb.b], writes=[dcol.b])
                    ebs.append(eb)
                    ed = Ft
                    S.op("dve", lambda h, ed=ed, eb=eb: h.reciprocal(out=ed[:], in_=eb[:]), reads=[eb.b], writes=[ed.b])
                    S.op("pool", lambda h, hd=hd, fm1=fm1, ed=ed: h.tensor_tensor(out=kbn[:, hd, :], in0=fm1[:], in1=ed[:], op=ALU.mult),
                         reads=[fm1.b, ed.b], writes=[kbn.b])
                    S.op("pool", lambda h, ed=ed, eb3=eb3: h.tensor_tensor(
                        out=ed[:].rearrange("p (c j) -> p c j", j=CH), in0=ed[:].rearrange("p (c j) -> p c j", j=CH),
                        in1=eb3[:, :, CH - 1:CH].to_broadcast([P, TB // CH, CH]), op=ALU.mult), reads=[eb.b, ed.b], writes=[ed.b])
                    kd = kdn[hd]
                    S.op("pool", lambda h, kd=kd, fm1=fm1, ed=ed: h.tensor_tensor(out=kd[:], in0=fm1[:], in1=ed[:], op=ALU.mult),
                         reads=[fm1.b, ed.b], writes=[kd.b])

            head_prep((0,))
            sgA = [X[2], X[3]]
            for cp in range(2):
                gate_proj(OAG + 2 * cp * P, sgA[cp])
            for cp in range(2):
                bank = PJ.next()
                for hh in range(2):
                    proj_fm(bank, hh, OA + (2 * cp + hh) * P)
                S.op("dve", lambda h, bank=bank, cp=cp: h.tensor_tensor(
                    out=U[:, 2 * cp:2 * cp + 2, 30:30 + TB], in0=split2(bank), in1=sgA[cp][:], op=ALU.mult),
                    reads=[bank.b, sgA[cp].b], writes=[U.b])
            head_prep((1,))
            for c in range(4):
                for q4 in range(4):
                    q = 4 * c + q4
                    bank = CV.next()
                    for j in range(4):
                        mm(bank, bank[32 * j:32 * j + 32, 0:RW], ident[:, 32 * q4:32 * q4 + 32], U[:, c, j:j + RW],
                           [ident.b, U.b], True, True, tile_position=(0, 32 * j))
                    S.op("act", lambda h, bank=bank, q=q: h.activation(out=R[:, q, :], in_=bank[:, 0:RW], func=AF.Copy),
                         reads=[bank.b], writes=[Rb[q]])
                if c == 1:
                    head_prep((2, 3))
            for j in range(NT):
                bank = PJ.next()
                for kc in range(KC):
                    mm(bank, bank[:], hT[:, kc, j * P:(j + 1) * P], Win[:, kc, OHI:OHI + 512], [WinB[OHI // 512], hT.b], kc == 0, kc == KC - 1)
                S.op("act", lambda h, bank=bank, j=j: h.activation(out=vn[j][:], in_=bank[:], func=AF.Copy, scale=-1.0),
                     reads=[bank.b], writes=[vn[j].b])
            cvbank = [None]
            for c in range(4):
                conv_chunk(c, cvbank)
            S.op("pool", lambda h: h.tensor_copy(out=U[:, :, 0:30], in_=U[:, :, TB:TB + 30]), reads=[U.b], writes=[U.b])
            for hp in range(2):
                bank = PJ.next()
                for hh in range(2):
                    proj_fm(bank, hh, OHQ + (2 * hp + hh) * P)
                S.op("dve", lambda h, bank=bank, hp=hp: h.tensor_tensor(out=qb[:, 2 * hp, :], in0=bank[:, 0:TB], in1=EB[2 * hp][:], op=ALU.mult),
                     reads=[bank.b, EB[2 * hp].b], writes=[qb.b])
                S.op("dve", lambda h, bank=bank, hp=hp: h.tensor_tensor(out=qb[:, 2 * hp + 1, :], in0=bank[:, TB:2 * TB], in1=EB[2 * hp + 1][:], op=ALU.mult),
                     reads=[bank.b, EB[2 * hp + 1].b], writes=[qb.b])
            trbank = PJ.next()
            trv = trbank[:].bitcast(BF16)
            for hd in range(NH):
                for j in range(NT):
                    S.op("pe", lambda h, hd=hd, j=j: h.transpose(
                        out=trv[:, (hd * NT + j) * P:(hd * NT + j + 1) * P], in_=kdn[hd][:, j * P:(j + 1) * P], identity=ident[:]),
                        reads=[kdn[hd].b, ident.b], writes=[trbank.b])
            S.op("act", lambda h: h.activation(out=kdT[:], in_=trv.rearrange("p (a t) -> p a t", a=NH * NT), func=AF.Copy),
                 reads=[trbank.b], writes=[kdT.b])
            for hp in range(2):
                bank = PJ.next()
                for hh in range(2):
                    hd = 2 * hp + hh
                    for j in range(NT):
                        q = hh * NT + j
                        mm(bank, bank[:, q * P:(q + 1) * P], kbn[:, hd, j * P:(j + 1) * P], qb[:, hd, j * P:(j + 1) * P],
                           [kbn.b, qb.b], True, True)
                S.op("dve", lambda h, bank=bank, hp=hp: h.tensor_tensor(
                    out=sc[:, hp * 4:hp * 4 + 4, :], in0=bank[:].rearrange("p (a t) -> p a t", a=4),
                    in1=maskt[:].unsqueeze(1).to_broadcast([P, 4, P]), op=ALU.mult),
                    reads=[bank.b, maskt.b], writes=[sc.b])
            stb = CV.next()
            for c in range(4):
                mm(stb, stb[:, 0:TB], ones2[:, 0:P], ybf[:, c, :], [ones2.b, ybf.b], c == 0, c == 3)
            for c in range(4):
                mm(stb, stb[:, TB:2 * TB], ones2[:, 0:P], ysq[:, c, :], [ones2.b, ysq.b], c == 0, c == 3)
            S.op("act", lambda h: h.activation(out=lnm[:], in_=stb[:, 0:TB], func=AF.Copy), reads=[stb.b], writes=[lnm.b])
            S.op("dve", lambda h: h.tensor_tensor(out=lnv[:], in0=stb[:, 0:TB], in1=lnm[:], op=ALU.mult), reads=[stb.b, lnm.b], writes=[lnv.b])
            S.op("dve", lambda h: h.scalar_tensor_tensor(out=lnv[:], in0=stb[:, TB:2 * TB], scalar=pcol(C_EPSL), in1=lnv[:], op0=ALU.add, op1=ALU.subtract),
                 reads=[stb.b, lnv.b, par.b], writes=[lnv.b])
            S.op("act", lambda h: h.activation(out=lnr[:], in_=lnv[:], func=AF.Ln), reads=[lnv.b], writes=[lnr.b])
            S.op("act", lambda h: h.activation(out=lnr[:], in_=lnr[:], func=AF.Exp, scale=-0.5), reads=[lnr.b], writes=[lnr.b])
            S.op("pool", lambda h: h.tensor_tensor(out=y32[:], in0=y32[:], in1=lnm[:].unsqueeze(1).to_broadcast([P, 4, TB]), op=ALU.subtract),
                 reads=[y32.b, lnm.b], writes=[y32.b])
            S.op("pool", lambda h: h.tensor_tensor(out=y32[:], in0=y32[:], in1=lnr[:].unsqueeze(1).to_broadcast([P, 4, TB]), op=ALU.mult),
                 reads=[y32.b, lnr.b], writes=[y32.b])
            for c in range(TB // CH):
                j = c // 2
                p0 = (c % 2) * CH
                cur = scur[0]
                nxt = 1 - cur
                for hd in range(NH):
                    mm(PSL, PSL[:, hd * P:(hd + 1) * P], kdT[p0:p0 + CH, hd * NT + j, :], vn[j][p0:p0 + CH, hd * P:(hd + 1) * P],
                       [kdT.b, vn[j].b], True, True)
                for hd in range(NH):
                    S.op("dve", lambda h, hd=hd, cur=cur, nxt=nxt, c=c: h.scalar_tensor_tensor(
                        out=Sst[hd][nxt][:], in0=Sst[hd][cur][:], scalar=dcol[:, hd * 4 + c:hd * 4 + c + 1], in1=PSL[:, hd * P:(hd + 1) * P],
                        op0=ALU.mult, op1=ALU.add), reads=[Sst[hd][cur].b, dcol.b, PSL.b], writes=[Sst[hd][nxt].b])
                    S.op("act", lambda h, hd=hd, nxt=nxt: h.activation(out=Sbf[hd][nxt][:], in_=Sst[hd][nxt][:], func=AF.Copy),
                         reads=[Sst[hd][nxt].b], writes=[Sbf[hd][nxt].b])
                for hd in range(NH):
                    ot = OT[hd // 2]
                    oc = (hd % 2) * TB + c * CH
                    mm(ot, ot[:, oc:oc + CH], Sbf[hd][cur][:], qb[:, hd, c * CH:(c + 1) * CH], [Sbf[hd][cur].b, qb.b], True, False)
                    mm(ot, ot[:, oc:oc + CH], vn[j][p0:p0 + CH, hd * P:(hd + 1) * P], sc[p0:p0 + CH, hd * NT + j, p0:p0 + CH],
                       [vn[j].b, sc.b], False, True)
                scur[0] = nxt
                if c < 2:
                    gbank = gate_proj(OHG + 2 * c * P, X[4 + c])
                    S.op("dve", lambda h, gbank=gbank, c=c: h.tensor_tensor(out=X[4 + c][:], in0=split2(gbank), in1=X[4 + c][:], op=ALU.mult),
                         reads=[gbank.b, X[4 + c].b], writes=[X[4 + c].b])
                elif c == 2:
                    gate_proj(OGA, X[2])
                else:
                    gate_proj(OGB, X[3])

        def ln_affine(n):
            for cp in range(2):
                for hh in range(2):
                    c = 2 * cp + hh
                    S.op("act", lambda h, c=c, hh=hh: h.activation(out=sgz[:, hh, :], in_=y32[:, c, :], func=AF.Sigmoid,
                                                                scale=pcol(C_LNW + c), bias=pcol(C_LNB + c)),
                         reads=[y32.b, par.b], writes=[sgz.b])
                    S.op("dve", lambda h, c=c, hh=hh: h.tensor_scalar(out=zt[:, hh, :], in0=y32[:, c, :], scalar1=pcol(C_LNW + c),
                                                                   scalar2=pcol(C_LNB + c), op0=ALU.mult, op1=ALU.add),
                         reads=[y32.b, par.b], writes=[zt.b])
                S.op("dve", lambda h, cp=cp: h.tensor_tensor(out=vt[:, 2 * cp:2 * cp + 2, :], in0=zt[:], in1=sgz[:], op=ALU.mult),
                     reads=[zt.b, sgz.b], writes=[vt.b])

        def outstage(n):
            osqs = [B512.next(), Tl(sc.ap.rearrange("p a t -> p (a t)")[:, 0:2 * TB].rearrange("p (a t) -> p a t", a=2), sc.b)]
            ssbs = []
            for hp in range(2):
                S.op("act", lambda h, hp=hp: h.activation(out=osqs[hp][:], in_=split2(OT[hp]), func=AF.Square),
                     reads=[OT[hp].b], writes=[osqs[hp].b])
            for hp in range(2):
                ssb = CV.next()
                for hh in range(2):
                    mm(ssb, ssb[:, hh * TB:(hh + 1) * TB], ones2[:, P:2 * P], osqs[hp][:, hh, :], [ones2.b, osqs[hp].b], True, True)
                ssbs.append(ssb)
            sds = [X[6], X[0]]
            for hp in range(2):
                S.op("act", lambda h, hp=hp: h.activation(out=sds[hp][:], in_=split2(ssbs[hp]), func=AF.Ln, bias=pcol(C_EPSN)),
                     reads=[ssbs[hp].b, par.b], writes=[sds[hp].b])
            for hp in range(2):
                S.op("act", lambda h, hp=hp: h.activation(out=sds[hp][:], in_=sds[hp][:], func=AF.Exp, scale=-0.5),
                     reads=[sds[hp].b], writes=[sds[hp].b])
            for hp in range(2):
                ot = OT[hp]
                sd = sds[hp]
                gg = X[4 + hp]
                S.op("dve", lambda h, gg=gg, sd=sd: h.tensor_tensor(out=gg[:], in0=gg[:], in1=sd[:], op=ALU.mult),
                     reads=[gg.b, sd.b], writes=[gg.b])
                for hh in range(2):
                    hd = 2 * hp + hh
                    S.op("dve", lambda h, hd=hd, hh=hh, ot=ot, gg=gg: h.scalar_tensor_tensor(
                        out=orn[:, hd, :], in0=ot[:, hh * TB:(hh + 1) * TB], scalar=pcol(C_HNW + hd), in1=gg[:, hh, :], op0=ALU.mult, op1=ALU.mult),
                        reads=[ot.b, gg.b, par.b], writes=[orn.b])

        def conv_branch(n):
            GA = [X[2], X[1], X[2], X[1]]
            for dp in range(4):
                sga = GA[dp]
                if dp > 0:
                    gate_proj(OGA + 2 * dp * P, sga)
                ycb = PJ.next()
                for hh in range(2):
                    dc = 2 * dp + hh
                    for c in range(4):
                        mm(ycb, ycb[:, hh * TB:(hh + 1) * TB], Wco[:, c, dc * P:(dc + 1) * P], vt[:, c, :], [Wco.b, vt.b], c == 0, c == 3)
                for hh in range(2):
                    dc = 2 * dp + hh
                    S.op("dve", lambda h, ycb=ycb, sga=sga, hh=hh, dc=dc: h.scalar_tensor_tensor(
                        out=yT[:, dc, :], in0=ycb[:, hh * TB:(hh + 1) * TB], scalar=pcol(C_BCO + dc), in1=sga[:, hh, :], op0=ALU.add, op1=ALU.mult),
                        reads=[ycb.b, sga.b, par.b], writes=[yT.b])

        def rec_branch(n):
            GB = [X[3], X[6], X[3], X[6]]
            for dp in range(4):
                sgb = GB[dp]
                if dp > 0:
                    gate_proj(OGB + 2 * dp * P, sgb)
                yrb = PJ.next()
                for hh in range(2):
                    dc = 2 * dp + hh
                    for hd in range(NH):
                        mm(yrb, yrb[:, hh * TB:(hh + 1) * TB], Who[:, hd, dc * P:(dc + 1) * P], orn[:, hd, :], [Who.b, orn.b], hd == 0, hd == NH - 1)
                S.op("dve", lambda h, yrb=yrb, sgb=sgb: h.tensor_tensor(out=sgb[:], in0=split2(yrb), in1=sgb[:], op=ALU.mult),
                     reads=[yrb.b, sgb.b], writes=[sgb.b])
                S.op("dve", lambda h, sgb=sgb, dp=dp: h.tensor_tensor(out=yT[:, 2 * dp:2 * dp + 2, :], in0=yT[:, 2 * dp:2 * dp + 2, :], in1=sgb[:], op=ALU.add),
                     reads=[sgb.b, yT.b], writes=[yT.b])

        def tailA(n):
            t0 = n * TB
            for j in range(NT):
                S.dma("sp", lambda h, j=j: h.dma_start(out=xin[j][:], in_=x_d[t0 + j * P:t0 + (j + 1) * P, :]), writes=[xin[j].b])
                for g in range(2):
                    bank = PJ.next()
                    for dc in range(KC):
                        mm(bank, bank[:], yT[:, dc, j * P:(j + 1) * P], Wout[:, dc, g * 512:(g + 1) * 512], [yT.b, WoutB[g]], dc == 0, dc == KC - 1)
                    S.op("dve", lambda h, bank=bank, j=j, g=g: h.tensor_tensor(out=xin[j][:, g * 512:(g + 1) * 512], in0=bank[:],
                                                                              in1=xin[j][:, g * 512:(g + 1) * 512], op=ALU.add),
                         reads=[bank.b, xin[j].b], writes=[xin[j].b])
                S.dma("sp", lambda h, j=j: h.dma_start(out=x1_d[t0 + j * P:t0 + (j + 1) * P, :], in_=xin[j][:]),
                      reads=[xin[j].b], writes=[x1B[n * NT + j]])

        stage_a(0)
        stage_b(0)
        for n in range(NB):
            blockA(n)
            ln_affine(n)
            outstage(n)
            conv_branch(n)
            if n + 1 < NB:
                stage_a(n + 1)
            rec_branch(n)
            if n + 1 < NB:
                stage_b(n + 1)
            tailA(n)

        fenceB = S.fence_all()
        Bm = Bump(0, fenceB)
        Wup = Bm.alloc("Wup", [KC, DFF], BF16)
        WupB = [S.buf("wup%d" % g, fenceB) for g in range(DFF // 512)]
        Wdn = Bm.alloc("Wdn", [DFF // P, D], BF16)
        WdnB = [S.buf("wdn%d" % g, fenceB) for g in range(8)]
        w_up_v = w_up_d.rearrange("(kc p) c -> p kc c", p=P)
        w_dn_v = w_dn_d.rearrange("(fc p) c -> p fc c", p=P)
        for g in range(DFF // 512):
            S.dma("pool", lambda h, g=g: h.dma_start(out=Wup[:, :, g * 512:(g + 1) * 512], in_=w_up_v[:, :, g * 512:(g + 1) * 512]), writes=[WupB[g]])
        for g in range(8):
            S.dma("pool", lambda h, g=g: h.dma_start(out=Wdn[:, g * 4:(g + 1) * 4, :], in_=w_dn_v[:, g * 4:(g + 1) * 4, :]), writes=[WdnB[g]])
        nfw = Bm.alloc("nfw", [D], F32)
        S.dma("sp", lambda h: h.dma_start(out=nfw[:], in_=nfw_d), writes=[nfw.b])
        xinB = [[Bm.alloc("xinB%d_%d" % (s_, j), [D], F32) for j in range(NT)] for s_ in range(3)]
        junkF = Bm.alloc("junkF", [D], BF16)
        xnB = [Bm.alloc("xnB%d" % j, [D], BF16) for j in range(NT)]
        hT2 = [Bm.alloc("hT2_%d" % i, [KC, TB], BF16) for i in range(2)]
        hid = Bm.alloc("hid", [DFF // P, TB], BF16)
        hidB = [S.buf("hid%d" % i, fenceB) for i in range(DFF // P // 2)]
        R512 = Rot([Bm.alloc("r512_%d" % i, [2, TB], F32) for i in range(4)])
        print("phase B arena bytes", Bm.hi)

        def stageB_a(n):
            t0 = n * TB
            xs = xinB[n % 3]
            for j in range(NT):
                S.dma("sp", lambda h, j=j: h.dma_start(out=xs[j][:], in_=x1_d[t0 + j * P:t0 + (j + 1) * P, :]),
                      reads=[x1B[n * NT + j]], writes=[xs[j].b])
            S.op("dve", lambda h: h.memset(cols[:, 0:NT], 0.0), writes=[cols.b])
            for j in range(NT):
                S.op("act", lambda h, j=j: h.activation(out=xnB[j][:], in_=xs[j][:], func=AF.Square, accum_out=cols[:, j:j + 1]),
                     reads=[xs[j].b, cols.b], writes=[xnB[j].b, cols.b])
            S.op("dve", lambda h: h.tensor_scalar(out=cols[:, 2:2 + NT], in0=cols[:, 0:NT], scalar1=1.0 / D, scalar2=pcol(C_EPSN), op0=ALU.mult, op1=ALU.add),
                 reads=[cols.b, par.b], writes=[cols.b])
            S.op("pool", lambda h: h.tensor_tensor(out=cols[:, 4:4 + NT], in0=cols[:, 2:2 + NT], in1=cexp[:, 1:2].to_broadcast([P, NT]), op=ALU.pow),
                 reads=[cols.b, cexp.b], writes=[cols.b])
            for j in range(NT):
                S.op("act", lambda h, j=j: h.activation(out=xnB[j][:], in_=xs[j][:], func=AF.Copy, scale=cols[:, 4 + j:5 + j]),
                     reads=[xs[j].b, cols.b], writes=[xnB[j].b])

        def stageB_b(n):
            hT2n = hT2[n % 2]
            for j in range(NT):
                bank = PB.next()
                bv = bank[:].bitcast(BF16)
                for kc in range(KC):
                    S.op("pe", lambda h, kc=kc, bv=bv, j=j: h.transpose(out=bv[:, kc * P:(kc + 1) * P], in_=xnB[j][:, kc * P:(kc + 1) * P], identity=ident[:]),
                         reads=[xnB[j].b, ident.b], writes=[bank.b])
                S.op("dve", lambda h, j=j, bv=bv: h.tensor_tensor(
                    out=hT2n[:, :, j * P:(j + 1) * P], in0=bv.rearrange("p (k t) -> p k t", k=KC),
                    in1=par[:, C_NW2:C_NW2 + KC].unsqueeze(2).to_broadcast([P, KC, P]), op=ALU.mult),
                    reads=[bank.b, par.b], writes=[hT2n.b])

        def upB(n, fps):
            hT2n = hT2[n % 2]
            for fp in fps:
                bank = PB.next()
                for hh in range(2):
                    fc = 2 * fp + hh
                    for kc in range(KC):
                        mm(bank, bank[:, hh * TB:(hh + 1) * TB], Wup[:, kc, fc * P:(fc + 1) * P], hT2n[:, kc, :],
                           [WupB[fc * P // 512], hT2n.b], kc == 0, kc == KC - 1)
                r = R512.next()
                S.op("act", lambda h, bank=bank, r=r: h.activation(out=r[:], in_=split2(bank), func=AF.Relu),
                     reads=[bank.b], writes=[r.b])
                eng = "dve" if fp % 2 == 0 else "pool"
                S.op(eng, lambda h, r=r, fp=fp: h.tensor_tensor(out=hid[:, 2 * fp:2 * fp + 2, :], in0=r[:], in1=r[:], op=ALU.mult),
                     reads=[r.b], writes=[hidB[fp]])

        def downB(n):
            t0 = n * TB
            xs = xinB[n % 3]
            S.op("dve", lambda h: h.memset(colsF[:, 0:NT], 0.0), writes=[colsF.b])
            for j in range(NT):
                for g in range(2):
                    bank = PB.next()
                    for fc in range(DFF // P):
                        mm(bank, bank[:], hid[:, fc, j * P:(j + 1) * P], Wdn[:, fc, g * 512:(g + 1) * 512], [hidB[fc // 2], WdnB[fc // 4]],
                           fc == 0, fc == DFF // P - 1)
                    S.op("dve", lambda h, bank=bank, j=j, g=g: h.tensor_tensor(out=xs[j][:, g * 512:(g + 1) * 512], in0=bank[:],
                                                                              in1=xs[j][:, g * 512:(g + 1) * 512], op=ALU.add),
                         reads=[bank.b, xs[j].b], writes=[xs[j].b])
                S.op("act", lambda h, j=j: h.activation(out=junkF[:], in_=xs[j][:], func=AF.Square, accum_out=colsF[:, j:j + 1]),
                     reads=[xs[j].b, colsF.b], writes=[junkF.b, colsF.b])
                S.op("dve", lambda h, j=j: h.tensor_scalar(out=colsF[:, 2 + j:3 + j], in0=colsF[:, j:j + 1], scalar1=1.0 / D, scalar2=pcol(C_EPSN), op0=ALU.mult, op1=ALU.add),
                     reads=[colsF.b, par.b], writes=[colsF.b])
                S.op("pool", lambda h, j=j: h.tensor_tensor(out=colsF[:, 4 + j:5 + j], in0=colsF[:, 2 + j:3 + j], in1=cexp[:, 1:2], op=ALU.pow),
                     reads=[colsF.b, cexp.b], writes=[colsF.b])
                S.op("dve", lambda h, j=j: h.scalar_tensor_tensor(out=xs[j][:], in0=xs[j][:], scalar=colsF[:, 4 + j:5 + j], in1=nfw[:],
                                                                   op0=ALU.mult, op1=ALU.mult),
                     reads=[xs[j].b, colsF.b, nfw.b], writes=[xs[j].b])
                S.dma("sp", lambda h, j=j: h.dma_start(out=y_d[t0 + j * P:t0 + (j + 1) * P, :], in_=xs[j][:]), reads=[xs[j].b], writes=[S.buf()])

        NFP = DFF // P // 2
        for n in range(min(2, NB)):
            stageB_a(n)
            stageB_b(n)
        for n in range(NB):
            upB(n, range(0, NFP // 2))
            if n + 2 < NB:
                stageB_a(n + 2)
            upB(n, range(NFP // 2, NFP))
            if n + 2 < NB:
                stageB_b(n + 2)
            downB(n)
        S.run()
    return nc


def host_consts(inp):
    par = np.zeros((P, NPAR), np.float32)
    dw = np.asarray(inp["dw_conv_w"], np.float32)[0, :, 0, :]
    for c in range(4):
        par[:, C_DW + c * KW:C_DW + (c + 1) * KW] = dw[:, c * P:(c + 1) * P].T
    col = lambda v, n: np.asarray(v, np.float32).reshape(n, P).T
    par[:, C_DWB:C_DWB + 4] = col(inp["dw_conv_b"][0], 4)
    par[:, C_LNW:C_LNW + 4] = col(inp["conv_ln_w"][0], 4)
    par[:, C_LNB:C_LNB + 4] = col(inp["conv_ln_b"][0], 4)
    par[:, C_BCO:C_BCO + 8] = col(inp["b_conv_out"][0], 8)
    par[:, C_LB0:C_LB0 + 4] = col(inp["hgrn_lb"][0], 4)
    par[:, C_LB1:C_LB1 + 4] = col(inp["hgrn_lb"][1], 4)
    par[:, C_HNW:C_HNW + 4] = col(inp["hgrn_norm_w"][0], 4)
    par[:, C_NW1:C_NW1 + 8] = col(inp["norm_mix_w"][0], 8)
    par[:, C_NW2:C_NW2 + 8] = col(inp["norm_mlp_w"][0], 8)
    par[:, C_EPSN] = 1e-6
    par[:, C_EPSL] = 1e-5
    nfw = np.ascontiguousarray(np.broadcast_to(np.asarray(inp["norm_final_w"], np.float32).reshape(1, D), (P, D)))
    ident = np.eye(P, dtype=np.float32).astype(ml_dtypes.bfloat16)
    ones2 = np.concatenate([np.full((P, P), 1.0 / 512, np.float32), np.full((P, P), 1.0 / 128, np.float32)], 1).astype(ml_dtypes.bfloat16)
    s = np.arange(P)[:, None]
    t = np.arange(P)[None, :]
    mask = ((s // CH == t // CH) & (s <= t)).astype(np.float32).astype(ml_dtypes.bfloat16)
    wd = np.zeros((4, 32, 16, 8, 32), np.float32)
    ci = np.arange(32)
    for q in range(16):
        for g in range(8):
            for j in range(4):
                if 4 * g + j < KW:
                    wd[j, ci, q, g, ci] = dw[4 * g + j, q * 32 + ci]
    wd = np.ascontiguousarray(wd.reshape(P, 16 * 8 * 32))
    return dict(params=par, nfw=nfw, ident=ident, ones2=ones2, mask=mask, wd=wd)


_NC_CACHE = {}


def run(inp, T, ncores):
    if T not in _NC_CACHE:
        _NC_CACHE[T] = build(T)
    nc = _NC_CACHE[T]
    c = host_consts(inp)
    shared = dict(
        w_in=np.ascontiguousarray(np.asarray(inp["w_in"], np.float32)[0]),
        w_conv_out=np.ascontiguousarray(np.asarray(inp["w_conv_out"], np.float32)[0]),
        w_hgrn_out=np.ascontiguousarray(np.asarray(inp["w_hgrn_out"], np.float32)[0]),
        w_out=np.ascontiguousarray(np.asarray(inp["w_out"], np.float32)[0]),
        w_mlp_up=np.ascontiguousarray(np.asarray(inp["w_mlp_up"], np.float32)[0]),
        w_mlp_down=np.ascontiguousarray(np.asarray(inp["w_mlp_down"], np.float32)[0]),
        **c)
    x = np.asarray(inp["x"], np.float32)
    in_maps = [dict(shared, x=np.ascontiguousarray(x[b])) for b in range(ncores)]
    res = run_bass_kernel_spmd(nc, in_maps, core_ids=list(range(ncores)))
    return np.stack([np.asarray(r["y"], np.float32) for r in res.results], 0)


def kernel(**inputs):
    return run(inputs, SEQ, NCORES)
```

```python
from contextlib import ExitStack
import numpy as np
import ml_dtypes
import concourse.bass as bass
import concourse.mybir as mybir
from concourse.bass_utils import run_bass_kernel_spmd

F32 = mybir.dt.float32
BF16 = mybir.dt.bfloat16
AF = mybir.ActivationFunctionType
ALU = mybir.AluOpType

P = 128
D = 1024
SEQ = 4096
NCORES = 8
TB = 256
NT = TB // P
KW = 31
NH = 4
CH = 64
DFF = 4096
INC = 5120
OA, OAG, OHQ, OHF, OHI, OHG, OGA, OGB = 0, 512, 1024, 1536, 2048, 2560, 3072, 4096
KC = D // P
C_DW, C_DWB, C_LNW, C_LNB, C_BCO, C_LB0, C_LB1, C_HNW, C_NW1, C_NW2, C_EPSN, C_EPSL = \
    0, 124, 128, 132, 136, 144, 148, 152, 156, 164, 172, 173
NPAR = 176


class Buf:
    __slots__ = ("name", "w", "r")

    def __init__(self, name, r=None):
        self.name = name
        self.w = None
        self.r = list(r) if r else []


class Tl:
    __slots__ = ("ap", "b")

    def __init__(self, ap, b):
        self.ap = ap
        self.b = b

    def __getitem__(self, k):
        return self.ap[k]


class Rot:
    def __init__(self, tiles):
        self.t = tiles
        self.i = 0

    def next(self):
        t = self.t[self.i % len(self.t)]
        self.i += 1
        return t


class Sched:
    ENGS = ("pe", "act", "dve", "pool", "sp")

    def __init__(self, nc, es, n_dma_sems=24):
        self.nc = nc
        self.ops = {e: [] for e in self.ENGS}
        self.sems = {}
        self.cnt = {}
        for e in self.ENGS:
            self.sems[e] = es.enter_context(nc.semaphore("s_" + e))
            self.cnt[e] = 0
        self.dma_sems = {"sp": [], "pool": []}
        for q, nq in (("sp", n_dma_sems), ("pool", 16)):
            for i in range(nq):
                k = "d%s%d" % (q, i)
                self.sems[k] = es.enter_context(nc.semaphore("s_" + k))
                self.cnt[k] = 0
                self.dma_sems[q].append(k)
        self.dma_rr = {"sp": 0, "pool": 0}
        self.seen = {e: {} for e in self.ENGS}
        self.nb = 0
        self.targets = {e: set() for e in self.ENGS}
        self.lastc = {e: 0 for e in self.ENGS}

    def buf(self, name=None, fence=None):
        self.nb += 1
        return Buf(name or "b%d" % self.nb, fence)

    def fence_all(self):
        f = [(e, self.lastc[e]) for e in self.ENGS if self.lastc[e] > 0]
        return f + [(k, self.cnt[k]) for q in ("sp", "pool") for k in self.dma_sems[q] if self.cnt[k] > 0]

    def _deps(self, eng, reads, writes):
        need = {}

        def add(tok):
            k, v = tok
            if need.get(k, 0) < v:
                need[k] = v
        for b in reads:
            if b.w is not None:
                add(b.w)
        for b in writes:
            if b.w is not None:
                add(b.w)
            for t in b.r:
                add(t)
        waits = []
        for k, v in need.items():
            if k == "pe" and eng == "pe":
                continue
            if self.seen[eng].get(k, 0) >= v:
                continue
            self.seen[eng][k] = v
            waits.append((k, v))
            if k in self.targets:
                self.targets[k].add(v)
        return waits

    def _commit(self, tok, reads, writes):
        for b in writes:
            b.w = tok
            b.r = []
        for b in reads:
            if b in writes:
                continue
            b.r.append(tok)
            if len(b.r) > 16:
                m = {}
                for k, v in b.r:
                    if m.get(k, 0) < v:
                        m[k] = v
                b.r = list(m.items())

    def op(self, eng, fn, reads=(), writes=()):
        waits = self._deps(eng, reads, writes)
        self.cnt[eng] += 1
        self.lastc[eng] = self.cnt[eng]
        tok = (eng, self.cnt[eng])
        self.ops[eng].append((waits, fn, self.cnt[eng], None))
        self._commit(tok, reads, writes)
        return tok

    def dma(self, eng, fn, reads=(), writes=()):
        waits = self._deps(eng, reads, writes)
        k = self.dma_sems[eng][self.dma_rr[eng] % len(self.dma_sems[eng])]
        self.dma_rr[eng] += 1
        prev = self.cnt[k]
        if prev > 0 and self.seen[eng].get(k, 0) < prev:
            self.seen[eng][k] = prev
            waits.append((k, prev))
        self.cnt[k] += 16
        tok = (k, self.cnt[k])
        self.cnt[eng] += 1
        self.ops[eng].append((waits, fn, self.cnt[eng], k))
        self._commit(tok, reads, writes)
        return tok

    def run(self):
        nc = self.nc
        sems = self.sems
        val = {}
        for e in self.ENGS:
            val[e] = {idx: i + 1 for i, idx in enumerate(sorted(self.targets[e]))}
        final = [(k, self.cnt[k]) for q in ("sp", "pool") for k in self.dma_sems[q] if self.cnt[k] > 0]
        ops, targets = self.ops, self.targets
        print("ops per engine", {e: len(ops[e]) for e in self.ENGS}, "sem incs", {e: len(targets[e]) for e in self.ENGS})

        def play(e, h):
            for waits, fn, idx, dsem in ops[e]:
                for k, v in waits:
                    h.wait_ge(sems[k], val[k][v] if k in val else v)
                ins = fn(h)
                if dsem is not None:
                    ins.then_inc(sems[dsem], 16)
                elif idx in targets[e]:
                    ins.then_inc(sems[e], 1)
            if e == "sp":
                for k, v in final:
                    h.wait_ge(sems[k], v)

        with nc.Block() as block:
            @block.tensor
            def _(h):
                play("pe", h)

            @block.scalar
            def _(h):
                play("act", h)

            @block.vector
            def _(h):
                play("dve", h)

            @block.gpsimd
            def _(h):
                play("pool", h)

            @block.sync
            def _(h):
                play("sp", h)


def build(T):
    NB = T // TB
    nc = bass.Bass("TRN2", target_bir_lowering=False)
    dram_in = lambda name, shape, dt: nc.dram_tensor(name, shape, dt, kind="ExternalInput").ap()
    x_d = dram_in("x", [T, D], F32)
    w_in_d = dram_in("w_in", [D, INC], F32)
    w_co_d = dram_in("w_conv_out", [512, D], F32)
    w_ho_d = dram_in("w_hgrn_out", [512, D], F32)
    w_out_d = dram_in("w_out", [D, D], F32)
    w_up_d = dram_in("w_mlp_up", [D, DFF], F32)
    w_dn_d = dram_in("w_mlp_down", [DFF, D], F32)
    par_d = dram_in("params", [P, NPAR], F32)
    nfw_d = dram_in("nfw", [P, D], F32)
    wd_d = dram_in("wd", [P, 16 * 8 * 32], F32)
    ident_d = dram_in("ident", [P, P], BF16)
    ones_d = dram_in("ones2", [P, 2 * P], BF16)
    mask_d = dram_in("mask", [P, P], BF16)
    y_d = nc.dram_tensor("y", [T, D], F32, kind="ExternalOutput").ap()
    x1_d = nc.dram_tensor("x1_scratch", [T, D], F32).ap()

    with ExitStack() as es:
        S = Sched(nc, es)
        ARENA_BYTES = 205 * 1024 + 512
        arena = es.enter_context(nc.sbuf_tensor("arena", [P, ARENA_BYTES // 2], BF16))

        OFFS = {}
        PITCH = ARENA_BYTES // 2

        class Bump:
            def __init__(self, start=0, fence=None):
                self.off = start
                self.fence = fence
                self.hi = start

            def alloc(self, name, free_shape, dt):
                n = int(np.prod(free_shape))
                nbytes = n * (4 if dt == F32 else 2)
                self.off = (self.off + 3) // 4 * 4
                a = arena[:, self.off // 2:(self.off + nbytes) // 2]
                if dt == F32:
                    a = a.bitcast(F32)
                if len(free_shape) == 2:
                    a = a.rearrange("p (a b) -> p a b", a=free_shape[0])
                elif len(free_shape) == 3:
                    a = a.rearrange("p (a b c) -> p a b c", a=free_shape[0], b=free_shape[1])
                OFFS[name] = self.off // 2
                self.off += nbytes
                self.hi = max(self.hi, self.off)
                assert self.off <= ARENA_BYTES, (name, self.off)
                return Tl(a, S.buf(name, self.fence))

        def small(name, shape, dt):
            t = es.enter_context(nc.sbuf_tensor(name, shape, dt))
            return Tl(t, S.buf(name))

        par = small("par", [P, NPAR], F32)
        ident = small("identb", [P, P], BF16)
        ones2 = small("ones2b", [P, 2 * P], BF16)
        maskt = small("maskt", [P, P], BF16)
        zeros = small("zeros", [P, CH], F32)
        lbA = small("lbA", [P, 12], F32)
        cols = small("cols", [P, 8], F32)
        colsF = small("colsF", [P, 8], F32)
        dcol = small("dcol", [P, 16], F32)
        cexp = small("cexp", [P, 2], F32)
        pcol = lambda c: par[:, c:c + 1]

        S.dma("sp", lambda h: h.dma_start(out=par[:], in_=par_d), writes=[par.b])
        S.dma("sp", lambda h: h.dma_start(out=ident[:], in_=ident_d), writes=[ident.b])
        S.dma("sp", lambda h: h.dma_start(out=ones2[:], in_=ones_d), writes=[ones2.b])
        S.dma("sp", lambda h: h.dma_start(out=maskt[:], in_=mask_d), writes=[maskt.b])
        S.op("dve", lambda h: h.memset(zeros[:], 0.0), writes=[zeros.b])
        S.op("dve", lambda h: h.memset(cexp[:, 0:1], -1.0), writes=[cexp.b])
        S.op("dve", lambda h: h.memset(cexp[:, 1:2], -0.5), writes=[cexp.b])
        S.op("dve", lambda h: h.tensor_tensor(out=lbA[:, 0:4], in0=par[:, C_LB0:C_LB0 + 4], in1=par[:, C_LB1:C_LB1 + 4],
                                              op=ALU.subtract), reads=[par.b], writes=[lbA.b])
        S.op("act", lambda h: h.activation(out=lbA[:, 0:4], in_=lbA[:, 0:4], func=AF.Sigmoid), reads=[lbA.b], writes=[lbA.b])
        S.op("dve", lambda h: h.tensor_scalar(out=lbA[:, 4:8], in0=lbA[:, 0:4], scalar1=-1.0, scalar2=1.0,
                                              op0=ALU.mult, op1=ALU.add), reads=[lbA.b], writes=[lbA.b])
        S.op("dve", lambda h: h.tensor_scalar(out=lbA[:, 8:12], in0=lbA[:, 0:4], scalar1=1.0, scalar2=-1.0,
                                              op0=ALU.mult, op1=ALU.add), reads=[lbA.b], writes=[lbA.b])

        psb = [es.enter_context(nc.psum_tensor("ps%d" % i, [P, 512], F32)) for i in range(8)]
        PJ = Rot([Tl(psb[i], S.buf("pj%d" % i)) for i in range(3)])
        CV = Rot([Tl(psb[i], S.buf("cv%d" % i)) for i in (3, 4)])
        OT = [Tl(psb[5], S.buf("ot0")), Tl(psb[6], S.buf("ot1"))]
        PSL = Tl(psb[7], S.buf("psl"))
        PB = Rot(PJ.t + CV.t)

        def mm(out_t, out_ap, lhsT, rhs, reads, start, stop, **kw):
            S.op("pe", lambda h: h.matmul(out=out_ap, lhsT=lhsT, rhs=rhs, start=start, stop=stop, **kw),
                 reads=reads, writes=[out_t.b])

        def split2(bank):
            return bank[:].rearrange("p (a t) -> p a t", a=2)

        A = Bump(0)
        Win = A.alloc("Win", [KC, INC], BF16)
        WinB = [S.buf("win%d" % g) for g in range(INC // 512)]
        Wco = A.alloc("Wco", [4, D], BF16)
        Who = A.alloc("Who", [4, D], BF16)
        Wout = A.alloc("Wout", [KC, D], BF16)
        WoutB = [S.buf("wout%d" % g) for g in range(2)]
        Wdt = A.alloc("Wdt", [16, 8, 32], BF16)
        w_in_v = w_in_d.rearrange("(kc p) c -> p kc c", p=P)
        for g in (OHF // 512, OAG // 512, OA // 512):
            S.dma("pool", lambda h, g=g: h.dma_start(out=Win[:, :, g * 512:(g + 1) * 512], in_=w_in_v[:, :, g * 512:(g + 1) * 512]),
                  writes=[WinB[g]])
        S.dma("pool", lambda h: h.dma_start(out=Wdt[:], in_=wd_d.rearrange("p (q g c) -> p q g c", q=16, g=8)), writes=[Wdt.b])
        for g in (OHQ // 512, OHI // 512, OHG // 512, 6, 7, 8, 9):
            S.dma("pool", lambda h, g=g: h.dma_start(out=Win[:, :, g * 512:(g + 1) * 512], in_=w_in_v[:, :, g * 512:(g + 1) * 512]),
                  writes=[WinB[g]])
        S.dma("pool", lambda h: h.dma_start(out=Wco[:], in_=w_co_d.rearrange("(kc p) c -> p kc c", p=P)), writes=[Wco.b])
        S.dma("pool", lambda h: h.dma_start(out=Who[:], in_=w_ho_d.rearrange("(kc p) c -> p kc c", p=P)), writes=[Who.b])
        w_out_v = w_out_d.rearrange("(kc p) c -> p kc c", p=P)
        for g in range(2):
            S.dma("pool", lambda h, g=g: h.dma_start(out=Wout[:, :, g * 512:(g + 1) * 512], in_=w_out_v[:, :, g * 512:(g + 1) * 512]),
                  writes=[WoutB[g]])

        UW = TB + 31
        RW = TB + 28
        xin = [A.alloc("xin%d" % j, [D], F32) for j in range(NT)]
        hT = A.alloc("hT", [KC, TB], BF16)
        U = A.alloc("U", [4, UW], BF16)
        R = A.alloc("R", [16, RW], BF16)
        Rb = [S.buf("R%d" % q) for q in range(16)]
        y32 = A.alloc("y32", [4, TB], F32)
        ysq = A.alloc("ysq", [4, TB], BF16)
        lnm = A.alloc("lnm", [TB], F32)
        lnv = A.alloc("lnv", [TB], F32)
        lnr = A.alloc("lnr", [TB], F32)
        zt = A.alloc("zt", [2, TB], F32)
        sgz = A.alloc("sgz", [2, TB], F32)
        vt = A.alloc("vt", [4, TB], BF16)
        T256 = Rot([A.alloc("t256_%d" % i, [TB], F32) for i in range(4)])
        EB = [A.alloc("eb%d" % i, [TB], F32) for i in range(NH)]
        X = [A.alloc("x512_%d" % i, [2, TB], F32) for i in range(7)]
        B512 = Rot([A.alloc("b512_%d" % i, [2, TB], BF16) for i in range(1)])
        qb = A.alloc("qb", [NH, TB], BF16)
        kbn = A.alloc("kbn", [NH, TB], BF16)
        kdn = [A.alloc("kdn%d" % i, [TB], BF16) for i in range(NH)]
        kdT = A.alloc("kdT", [NH * NT, P], BF16)
        vn = [A.alloc("vn%d" % j, [512], BF16) for j in range(NT)]
        sc = A.alloc("sc", [NH * NT, P], BF16)
        Sst = [[A.alloc("S%d_%d" % (h, i), [P], F32) for i in range(2)] for h in range(NH)]
        Sbf = [[A.alloc("Sb%d_%d" % (h, i), [P], BF16) for i in range(2)] for h in range(NH)]
        orn = A.alloc("orn", [NH, TB], BF16)
        yT = A.alloc("yT", [KC, TB], BF16)
        ybf = Tl(vt.ap, vt.b)
        xn = [Tl(sc.ap.rearrange("p a t -> p (a t)"), sc.b), Tl(kdT.ap.rearrange("p a t -> p (a t)"), kdT.b)]
        print("phase A arena bytes", A.hi, "of", ARENA_BYTES)

        for h_ in range(NH):
            S.op("dve", lambda h, t=Sst[h_][0]: h.memset(t[:], 0.0), writes=[Sst[h_][0].b])
            S.op("dve", lambda h, t=Sbf[h_][0]: h.memset(t[:], 0.0), writes=[Sbf[h_][0].b])
        S.op("dve", lambda h: h.memset(U[:], 0.0), writes=[U.b])
        scur = [0]
        x1B = [S.buf("x1d%d" % i) for i in range(NB * NT)]

        def proj_fm(bank, half, col0):
            for kc in range(KC):
                mm(bank, bank[:, half * TB:(half + 1) * TB], Win[:, kc, col0:col0 + P], hT[:, kc, :],
                   [WinB[col0 // 512], hT.b], kc == 0, kc == KC - 1)

        def stage_a(n):
            t0 = n * TB
            for j in range(NT):
                S.dma("sp", lambda h, j=j: h.dma_start(out=xin[j][:], in_=x_d[t0 + j * P:t0 + (j + 1) * P, :]), writes=[xin[j].b])
            S.op("dve", lambda h: h.memset(cols[:, 0:NT], 0.0), writes=[cols.b])
            for j in range(NT):
                S.op("act", lambda h, j=j: h.activation(out=xn[j][:], in_=xin[j][:], func=AF.Square, accum_out=cols[:, j:j + 1]),
                     reads=[xin[j].b, cols.b], writes=[xn[j].b, cols.b])
            S.op("dve", lambda h: h.tensor_scalar(out=cols[:, 2:2 + NT], in0=cols[:, 0:NT], scalar1=1.0 / D, scalar2=pcol(C_EPSN), op0=ALU.mult, op1=ALU.add),
                 reads=[cols.b, par.b], writes=[cols.b])
            S.op("pool", lambda h: h.tensor_tensor(out=cols[:, 4:4 + NT], in0=cols[:, 2:2 + NT], in1=cexp[:, 1:2].to_broadcast([P, NT]), op=ALU.pow),
                 reads=[cols.b, cexp.b], writes=[cols.b])
            for j in range(NT):
                S.op("act", lambda h, j=j: h.activation(out=xn[j][:], in_=xin[j][:], func=AF.Copy, scale=cols[:, 4 + j:5 + j]),
                     reads=[xin[j].b, cols.b], writes=[xn[j].b])

        def stage_b(n):
            for j in range(NT):
                bank = PJ.next()
                bv = bank[:].bitcast(BF16)
                for kc in range(KC):
                    S.op("pe", lambda h, kc=kc, bv=bv, j=j: h.transpose(out=bv[:, kc * P:(kc + 1) * P], in_=xn[j][:, kc * P:(kc + 1) * P], identity=ident[:]),
                         reads=[xn[j].b, ident.b], writes=[bank.b])
                S.op("dve", lambda h, j=j, bv=bv: h.tensor_tensor(
                    out=hT[:, :, j * P:(j + 1) * P], in0=bv.rearrange("p (k t) -> p k t", k=KC),
                    in1=par[:, C_NW1:C_NW1 + KC].unsqueeze(2).to_broadcast([P, KC, P]), op=ALU.mult),
                    reads=[bank.b, par.b], writes=[hT.b])

        def conv_chunk(c, cvbank):
            hh = c % 2
            if hh == 0:
                cvbank[0] = CV.next()
            bank = cvbank[0]
            for g in range(8):
                for q4 in range(4):
                    q = 4 * c + q4
                    mm(bank, bank[32 * q4:32 * q4 + 32, hh * TB:(hh + 1) * TB], Wdt[:, q, g, :], R[:, q, 4 * g:4 * g + TB],
                       [Wdt.b, Rb[q]], g == 0, g == 7, tile_position=(0, 32 * q4))
            if hh == 1:
                for h2 in range(2):
                    c2 = c - 1 + h2
                    S.op("act", lambda h, bank=bank, c2=c2, h2=h2: h.activation(
                        out=y32[:, c2, :], in_=bank[:, h2 * TB:(h2 + 1) * TB], func=AF.Identity, bias=pcol(C_DWB + c2)),
                        reads=[bank.b, par.b], writes=[y32.b])
                    S.op("act", lambda h, bank=bank, c2=c2, h2=h2: h.activation(
                        out=ysq[:, c2, :], in_=bank[:, h2 * TB:(h2 + 1) * TB], func=AF.Square, bias=pcol(C_DWB + c2)),
                        reads=[bank.b, par.b], writes=[ysq.b])
                    S.op("act", lambda h, bank=bank, c2=c2, h2=h2: h.activation(
                        out=ybf[:, c2, :], in_=bank[:, h2 * TB:(h2 + 1) * TB], func=AF.Identity, bias=pcol(C_DWB + c2)),
                        reads=[bank.b, par.b], writes=[ybf.b])

        def gate_proj(col0, dst):
            bank = PJ.next()
            for hh in range(2):
                proj_fm(bank, hh, col0 + hh * P)
            S.op("act", lambda h, bank=bank, dst=dst: h.activation(out=dst[:], in_=split2(bank), func=AF.Sigmoid),
                 reads=[bank.b], writes=[dst.b])
            return bank

        def blockA(n):
            hfb = [X[0], X[1]]
            for hp in range(2):
                gate_proj(OHF + 2 * hp * P, hfb[hp])
            def head_prep(HEADS):
                ebs = []
                for hd in HEADS:
                    hp, hh = hd // 2, hd % 2
                    sgf = hfb[hp]
                    Ft = T256.next()
                    S.op("act", lambda h, Ft=Ft, sgf=sgf, hh=hh, hd=hd: h.activation(
                        out=Ft[:], in_=sgf[:, hh, :], func=AF.Identity, scale=lbA[:, 4 + hd:5 + hd], bias=lbA[:, hd:hd + 1]),
                        reads=[sgf.b, lbA.b], writes=[Ft.b])
                    fm1 = T256.next()
                    S.op("act", lambda h, fm1=fm1, sgf=sgf, hh=hh, hd=hd: h.activation(
                        out=fm1[:], in_=sgf[:, hh, :], func=AF.Identity, scale=lbA[:, 4 + hd:5 + hd], bias=lbA[:, 8 + hd:9 + hd]),
                        reads=[sgf.b, lbA.b], writes=[fm1.b])
                    eb = EB[hd]
                    for c in range(TB // CH):
                        S.op("dve", lambda h, eb=eb, Ft=Ft, c=c: h.tensor_tensor_scan(
                            out=eb[:, c * CH:(c + 1) * CH], data0=Ft[:, c * CH:(c + 1) * CH], data1=zeros[:], initial=1.0,
                            op0=ALU.mult, op1=ALU.add), reads=[Ft.b, zeros.b], writes=[eb.b])
                    eb3 = eb[:].rearrange("p (c j) -> p c j", j=CH)
                    S.op("pool", lambda h, eb3=eb3, hd=hd: h.tensor_copy(out=dcol[:, hd * 4:hd * 4 + 4].unsqueeze(2), in_=eb3[:, :, CH - 1:CH]),
                         reads=[eb.b], writes=[dcol.b])
                    ebs.append(eb)
                    ed = Ft
                    S.op("dve", lambda h, ed=ed, eb=eb: h.reciprocal(out=ed[:], in_=eb[:]), reads=[eb.b], writes=[ed.b])
                    S.op("pool", lambda h, hd=hd, fm1=fm1, ed=ed: h.tensor_tensor(out=kbn[:, hd, :], in0=fm1[:], in1=ed[:], op=ALU.mult),
                         reads=[fm1.b, ed.b], writes=[kbn.b])
                    S.op("pool", lambda h, ed=ed, eb3=eb3: h.tensor_tensor(
                        out=ed[:].rearrange("p (c j) -> p c j", j=CH), in0=ed[:].rearrange("p (c j) -> p c j", j=CH),
                        in1=eb3[:, :, CH - 1:CH].to_broadcast([P, TB // CH, CH]), op=ALU.mult), reads=[eb.b, ed.b], writes=[ed.b])
                    kd = kdn[hd]
                    S.op("pool", lambda h, kd=kd, fm1=fm1, ed=ed: h.tensor_tensor(out=kd[:], in0=fm1[:], in1=ed[:], op=ALU.mult),
                         reads=[fm1.b, ed.b], writes=[kd.b])

            head_prep((0,))
            sgA = [X[2], X[3]]
            for cp in range(2):
                gate_proj(OAG + 2 * cp * P, sgA[cp])
            for cp in range(2):
                bank = PJ.next()
                for hh in range(2):
                    proj_fm(bank, hh, OA + (2 * cp + hh) * P)
                S.op("dve", lambda h, bank=bank, cp=cp: h.tensor_tensor(
                    out=U[:, 2 * cp:2 * cp + 2, 30:30 + TB], in0=split2(bank), in1=sgA[cp][:], op=ALU.mult),
                    reads=[bank.b, sgA[cp].b], writes=[U.b])
            head_prep((1,))
            for c in range(4):
                for q4 in range(4):
                    q = 4 * c + q4
                    bank = CV.next()
                    for j in range(4):
                        mm(bank, bank[32 * j:32 * j + 32, 0:RW], ident[:, 32 * q4:32 * q4 + 32], U[:, c, j:j + RW],
                           [ident.b, U.b], True, True, tile_position=(0, 32 * j))
                    S.op("act", lambda h, bank=bank, q=q: h.activation(out=R[:, q, :], in_=bank[:, 0:RW], func=AF.Copy),
                         reads=[bank.b], writes=[Rb[q]])
                if c == 1:
                    head_prep((2, 3))
            for j in range(NT):
                bank = PJ.next()
                for kc in range(KC):
                    mm(bank, bank[:], hT[:, kc, j * P:(j + 1) * P], Win[:, kc, OHI:OHI + 512], [WinB[OHI // 512], hT.b], kc == 0, kc == KC - 1)
                S.op("act", lambda h, bank=bank, j=j: h.activation(out=vn[j][:], in_=bank[:], func=AF.Copy, scale=-1.0),
                     reads=[bank.b], writes=[vn[j].b])
            cvbank = [None]
            for c in range(4):
                conv_chunk(c, cvbank)
            S.op("pool", lambda h: h.tensor_copy(out=U[:, :, 0:30], in_=U[:, :, TB:TB + 30]), reads=[U.b], writes=[U.b])
            for hp in range(2):
                bank = PJ.next()
                for hh in range(2):
                    proj_fm(bank, hh, OHQ + (2 * hp + hh) * P)
                S.op("dve", lambda h, bank=bank, hp=hp: h.tensor_tensor(out=qb[:, 2 * hp, :], in0=bank[:, 0:TB], in1=EB[2 * hp][:], op=ALU.mult),
                     reads=[bank.b, EB[2 * hp].b], writes=[qb.b])
                S.op("dve", lambda h, bank=bank, hp=hp: h.tensor_tensor(out=qb[:, 2 * hp + 1, :], in0=bank[:, TB:2 * TB], in1=EB[2 * hp + 1][:], op=ALU.mult),
                     reads=[bank.b, EB[2 * hp + 1].b], writes=[qb.b])
            trbank = PJ.next()
            trv = trbank[:].bitcast(BF16)
            for hd in range(NH):
                for j in range(NT):
                    S.op("pe", lambda h, hd=hd, j=j: h.transpose(
                        out=trv[:, (hd * NT + j) * P:(hd * NT + j + 1) * P], in_=kdn[hd][:, j * P:(j + 1) * P], identity=ident[:]),
                        reads=[kdn[hd].b, ident.b], writes=[trbank.b])
            S.op("act", lambda h: h.activation(out=kdT[:], in_=trv.rearrange("p (a t) -> p a t", a=NH * NT), func=AF.Copy),
                 reads=[trbank.b], writes=[kdT.b])
            for hp in range(2):
                bank = PJ.next()
                for hh in range(2):
                    hd = 2 * hp + hh
                    for j in range(NT):
                        q = hh * NT + j
                        mm(bank, bank[:, q * P:(q + 1) * P], kbn[:, hd, j * P:(j + 1) * P], qb[:, hd, j * P:(j + 1) * P],
                           [kbn.b, qb.b], True, True)
                S.op("dve", lambda h, bank=bank, hp=hp: h.tensor_tensor(
                    out=sc[:, hp * 4:hp * 4 + 4, :], in0=bank[:].rearrange("p (a t) -> p a t", a=4),
                    in1=maskt[:].unsqueeze(1).to_broadcast([P, 4, P]), op=ALU.mult),
                    reads=[bank.b, maskt.b], writes=[sc.b])
            stb = CV.next()
            for c in range(4):
                mm(stb, stb[:, 0:TB], ones2[:, 0:P], ybf[:, c, :], [ones2.b, ybf.b], c == 0, c == 3)
            for c in range(4):
                mm(stb, stb[:, TB:2 * TB], ones2[:, 0:P], ysq[:, c, :], [ones2.b, ysq.b], c == 0, c == 3)
            S.op("act", lambda h: h.activation(out=lnm[:], in_=stb[:, 0:TB], func=AF.Copy), reads=[stb.b], writes=[lnm.b])
            S.op("dve", lambda h: h.tensor_tensor(out=lnv[:], in0=stb[:, 0:TB], in1=lnm[:], op=ALU.mult), reads=[stb.b, lnm.b], writes=[lnv.b])
            S.op("dve", lambda h: h.scalar_tensor_tensor(out=lnv[:], in0=stb[:, TB:2 * TB], scalar=pcol(C_EPSL), in1=lnv[:], op0=ALU.add, op1=ALU.subtract),
                 reads=[stb.b, lnv.b, par.b], writes=[lnv.b])
            S.op("act", lambda h: h.activation(out=lnr[:], in_=lnv[:], func=AF.Ln), reads=[lnv.b], writes=[lnr.b])
            S.op("act", lambda h: h.activation(out=lnr[:], in_=lnr[:], func=AF.Exp, scale=-0.5), reads=[lnr.b], writes=[lnr.b])
            S.op("pool", lambda h: h.tensor_tensor(out=y32[:], in0=y32[:], in1=lnm[:].unsqueeze(1).to_broadcast([P, 4, TB]), op=ALU.subtract),
                 reads=[y32.b, lnm.b], writes=[y32.b])
            S.op("pool", lambda h: h.tensor_tensor(out=y32[:], in0=y32[:], in1=lnr[:].unsqueeze(1).to_broadcast([P, 4, TB]), op=ALU.mult),
                 reads=[y32.b, lnr.b], writes=[y32.b])
            for c in range(TB // CH):
                j = c // 2
                p0 = (c % 2) * CH
                cur = scur[0]
                nxt = 1 - cur
                for hd in range(NH):
                    mm(PSL, PSL[:, hd * P:(hd + 1) * P], kdT[p0:p0 + CH, hd * NT + j, :], vn[j][p0:p0 + CH, hd * P:(hd + 1) * P],
                       [kdT.b, vn[j].b], True, True)
                for hd in range(NH):
                    S.op("dve", lambda h, hd=hd, cur=cur, nxt=nxt, c=c: h.scalar_tensor_tensor(
                        out=Sst[hd][nxt][:], in0=Sst[hd][cur][:], scalar=dcol[:, hd * 4 + c:hd * 4 + c + 1], in1=PSL[:, hd * P:(hd + 1) * P],
                        op0=ALU.mult, op1=ALU.add), reads=[Sst[hd][cur].b, dcol.b, PSL.b], writes=[Sst[hd][nxt].b])
                    S.op("act", lambda h, hd=hd, nxt=nxt: h.activation(out=Sbf[hd][nxt][:], in_=Sst[hd][nxt][:], func=AF.Copy),
                         reads=[Sst[hd][nxt].b], writes=[Sbf[hd][nxt].b])
                for hd in range(NH):
                    ot = OT[hd // 2]
                    oc = (hd % 2) * TB + c * CH
                    mm(ot, ot[:, oc:oc + CH], Sbf[hd][cur][:], qb[:, hd, c * CH:(c + 1) * CH], [Sbf[hd][cur].b, qb.b], True, False)
                    mm(ot, ot[:, oc:oc + CH], vn[j][p0:p0 + CH, hd * P:(hd + 1) * P], sc[p0:p0 + CH, hd * NT + j, p0:p0 + CH],
                       [vn[j].b, sc.b], False, True)
                scur[0] = nxt
                if c < 2:
                    gbank = gate_proj(OHG + 2 * c * P, X[4 + c])
                    S.op("dve", lambda h, gbank=gbank, c=c: h.tensor_tensor(out=X[4 + c][:], in0=split2(gbank), in1=X[4 + c][:], op=ALU.mult),
                         reads=[gbank.b, X[4 + c].b], writes=[X[4 + c].b])
                elif c == 2:
                    gate_proj(OGA, X[2])
                else:
                    gate_proj(OGB, X[3])

        def ln_affine(n):
            for cp in range(2):
                for hh in range(2):
                    c = 2 * cp + hh
                    S.op("act", lambda h, c=c, hh=hh: h.activation(out=sgz[:, hh, :], in_=y32[:, c, :], func=AF.Sigmoid,
                                                                scale=pcol(C_LNW + c), bias=pcol(C_LNB + c)),
                         reads=[y32.b, par.b], writes=[sgz.b])
                    S.op("dve", lambda h, c=c, hh=hh: h.tensor_scalar(out=zt[:, hh, :], in0=y32[:, c, :], scalar1=pcol(C_LNW + c),
                                                                   scalar2=pcol(C_LNB + c), op0=ALU.mult, op1=ALU.add),
                         reads=[y32.b, par.b], writes=[zt.b])
                S.op("dve", lambda h, cp=cp: h.tensor_tensor(out=vt[:, 2 * cp:2 * cp + 2, :], in0=zt[:], in1=sgz[:], op=ALU.mult),
                     reads=[zt.b, sgz.b], writes=[vt.b])

        def outstage(n):
            osqs = [B512.next(), Tl(sc.ap.rearrange("p a t -> p (a t)")[:, 0:2 * TB].rearrange("p (a t) -> p a t", a=2), sc.b)]
            ssbs = []
            for hp in range(2):
                S.op("act", lambda h, hp=hp: h.activation(out=osqs[hp][:], in_=split2(OT[hp]), func=AF.Square),
                     reads=[OT[hp].b], writes=[osqs[hp].b])
            for hp in range(2):
                ssb = CV.next()
                for hh in range(2):
                    mm(ssb, ssb[:, hh * TB:(hh + 1) * TB], ones2[:, P:2 * P], osqs[hp][:, hh, :], [ones2.b, osqs[hp].b], True, True)
                ssbs.append(ssb)
            sds = [X[6], X[0]]
            for hp in range(2):
                S.op("act", lambda h, hp=hp: h.activation(out=sds[hp][:], in_=split2(ssbs[hp]), func=AF.Ln, bias=pcol(C_EPSN)),
                     reads=[ssbs[hp].b, par.b], writes=[sds[hp].b])
            for hp in range(2):
                S.op("act", lambda h, hp=hp: h.activation(out=sds[hp][:], in_=sds[hp][:], func=AF.Exp, scale=-0.5),
                     reads=[sds[hp].b], writes=[sds[hp].b])
            for hp in range(2):
                ot = OT[hp]
                sd = sds[hp]
                gg = X[4 + hp]
                S.op("pool", lambda h, gg=gg, sd=sd: h.tensor_tensor(out=gg[:], in0=gg[:], in1=sd[:], op=ALU.mult),
                     reads=[gg.b, sd.b], writes=[gg.b])
                for hh in range(2):
                    hd = 2 * hp + hh
                    S.op("dve", lambda h, hd=hd, hh=hh, ot=ot, gg=gg: h.scalar_tensor_tensor(
                        out=orn[:, hd, :], in0=ot[:, hh * TB:(hh + 1) * TB], scalar=pcol(C_HNW + hd), in1=gg[:, hh, :], op0=ALU.mult, op1=ALU.mult),
                        reads=[ot.b, gg.b, par.b], writes=[orn.b])

        def conv_branch(n):
            GA = [X[2], X[1], X[2], X[1]]
            for dp in range(4):
                sga = GA[dp]
                if dp > 0:
                    gate_proj(OGA + 2 * dp * P, sga)
                ycb = PJ.next()
                for hh in range(2):
                    dc = 2 * dp + hh
                    for c in range(4):
                        mm(ycb, ycb[:, hh * TB:(hh + 1) * TB], Wco[:, c, dc * P:(dc + 1) * P], vt[:, c, :], [Wco.b, vt.b], c == 0, c == 3)
                for hh in range(2):
                    dc = 2 * dp + hh
                    S.op("dve", lambda h, ycb=ycb, sga=sga, hh=hh, dc=dc: h.scalar_tensor_tensor(
                        out=yT[:, dc, :], in0=ycb[:, hh * TB:(hh + 1) * TB], scalar=pcol(C_BCO + dc), in1=sga[:, hh, :], op0=ALU.add, op1=ALU.mult),
                        reads=[ycb.b, sga.b, par.b], writes=[yT.b])

        def rec_branch(n):
            GB = [X[3], X[6], X[3], X[6]]
            for dp in range(4):
                sgb = GB[dp]
                if dp > 0:
                    gate_proj(OGB + 2 * dp * P, sgb)
                yrb = PJ.next()
                for hh in range(2):
                    dc = 2 * dp + hh
                    for hd in range(NH):
                        mm(yrb, yrb[:, hh * TB:(hh + 1) * TB], Who[:, hd, dc * P:(dc + 1) * P], orn[:, hd, :], [Who.b, orn.b], hd == 0, hd == NH - 1)
                S.op("dve", lambda h, yrb=yrb, sgb=sgb: h.tensor_tensor(out=sgb[:], in0=split2(yrb), in1=sgb[:], op=ALU.mult),
                     reads=[yrb.b, sgb.b], writes=[sgb.b])
                S.op("dve", lambda h, sgb=sgb, dp=dp: h.tensor_tensor(out=yT[:, 2 * dp:2 * dp + 2, :], in0=yT[:, 2 * dp:2 * dp + 2, :], in1=sgb[:], op=ALU.add),
                     reads=[sgb.b, yT.b], writes=[yT.b])

        def tailA(n):
            t0 = n * TB
            for j in range(NT):
                S.dma("sp", lambda h, j=j: h.dma_start(out=xin[j][:], in_=x_d[t0 + j * P:t0 + (j + 1) * P, :]), writes=[xin[j].b])
                for g in range(2):
                    bank = PJ.next()
                    for dc in range(KC):
                        mm(bank, bank[:], yT[:, dc, j * P:(j + 1) * P], Wout[:, dc, g * 512:(g + 1) * 512], [yT.b, WoutB[g]], dc == 0, dc == KC - 1)
                    S.op("dve", lambda h, bank=bank, j=j, g=g: h.tensor_tensor(out=xin[j][:, g * 512:(g + 1) * 512], in0=bank[:],
                                                                              in1=xin[j][:, g * 512:(g + 1) * 512], op=ALU.add),
                         reads=[bank.b, xin[j].b], writes=[xin[j].b])
                S.dma("sp", lambda h, j=j: h.dma_start(out=x1_d[t0 + j * P:t0 + (j + 1) * P, :], in_=xin[j][:]),
                      reads=[xin[j].b], writes=[x1B[n * NT + j]])

        fence_wup = [None]
        stage_a(0)
        stage_b(0)
        for n in range(NB):
            blockA(n)
            ln_affine(n)
            outstage(n)
            conv_branch(n)
            if n + 1 < NB:
                stage_a(n + 1)
            rec_branch(n)
            fence_wup[0] = [("pe", S.lastc["pe"])]
            if n + 1 < NB:
                stage_b(n + 1)
            tailA(n)

        fenceB = S.fence_all()
        Bm = Bump(0, fenceB)
        Wup = Bm.alloc("Wup", [KC, DFF], BF16)
        WupB = [S.buf("wup%d" % g, fence_wup[0]) for g in range(DFF // 512)]
        Wdn = Bm.alloc("Wdn", [DFF // P, D], BF16)
        WdnB = [S.buf("wdn%d" % g, fenceB) for g in range(8)]
        w_up_v = w_up_d.rearrange("(kc p) c -> p kc c", p=P)
        w_dn_v = w_dn_d.rearrange("(fc p) c -> p fc c", p=P)
        for g in range(DFF // 512):
            S.dma("pool", lambda h, g=g: h.dma_start(out=Wup[:, :, g * 512:(g + 1) * 512], in_=w_up_v[:, :, g * 512:(g + 1) * 512]), writes=[WupB[g]])
        for g in range(8):
            S.dma("pool", lambda h, g=g: h.dma_start(out=Wdn[:, g * 4:(g + 1) * 4, :], in_=w_dn_v[:, g * 4:(g + 1) * 4, :]), writes=[WdnB[g]])
        nfw = Bm.alloc("nfw", [D], F32)
        S.dma("sp", lambda h: h.dma_start(out=nfw[:], in_=nfw_d), writes=[nfw.b])
        xinB = [[Bm.alloc("xinB%d_%d" % (s_, j), [D], F32) for j in range(NT)] for s_ in range(3)]
        junkF = Bm.alloc("junkF", [D], BF16)
        xnB = [Bm.alloc("xnB%d" % j, [D], BF16) for j in range(NT)]
        hT2 = [Bm.alloc("hT2_%d" % i, [KC, TB], BF16) for i in range(2)]
        hid = Bm.alloc("hid", [DFF // P, TB], BF16)
        hidB = [S.buf("hid%d" % i, fenceB) for i in range(DFF // P // 2)]
        R512 = Rot([Bm.alloc("r512_%d" % i, [2, TB], F32) for i in range(4)])
        print("phase B arena bytes", Bm.hi)

        def stageB_a(n):
            t0 = n * TB
            xs = xinB[n % 3]
            for j in range(NT):
                S.dma("sp", lambda h, j=j: h.dma_start(out=xs[j][:], in_=x1_d[t0 + j * P:t0 + (j + 1) * P, :]),
                      reads=[x1B[n * NT + j]], writes=[xs[j].b])
            S.op("dve", lambda h: h.memset(cols[:, 0:NT], 0.0), writes=[cols.b])
            for j in range(NT):
                S.op("act", lambda h, j=j: h.activation(out=xnB[j][:], in_=xs[j][:], func=AF.Square, accum_out=cols[:, j:j + 1]),
                     reads=[xs[j].b, cols.b], writes=[xnB[j].b, cols.b])
            S.op("dve", lambda h: h.tensor_scalar(out=cols[:, 2:2 + NT], in0=cols[:, 0:NT], scalar1=1.0 / D, scalar2=pcol(C_EPSN), op0=ALU.mult, op1=ALU.add),
                 reads=[cols.b, par.b], writes=[cols.b])
            S.op("pool", lambda h: h.tensor_tensor(out=cols[:, 4:4 + NT], in0=cols[:, 2:2 + NT], in1=cexp[:, 1:2].to_broadcast([P, NT]), op=ALU.pow),
                 reads=[cols.b, cexp.b], writes=[cols.b])
            for j in range(NT):
                S.op("act", lambda h, j=j: h.activation(out=xnB[j][:], in_=xs[j][:], func=AF.Copy, scale=cols[:, 4 + j:5 + j]),
                     reads=[xs[j].b, cols.b], writes=[xnB[j].b])

        def stageB_b(n):
            hT2n = hT2[n % 2]
            for j in range(NT):
                bank = PB.next()
                bv = bank[:].bitcast(BF16)
                for kc in range(KC):
                    S.op("pe", lambda h, kc=kc, bv=bv, j=j: h.transpose(out=bv[:, kc * P:(kc + 1) * P], in_=xnB[j][:, kc * P:(kc + 1) * P], identity=ident[:]),
                         reads=[xnB[j].b, ident.b], writes=[bank.b])
                S.op("dve", lambda h, j=j, bv=bv: h.tensor_tensor(
                    out=hT2n[:, :, j * P:(j + 1) * P], in0=bv.rearrange("p (k t) -> p k t", k=KC),
                    in1=par[:, C_NW2:C_NW2 + KC].unsqueeze(2).to_broadcast([P, KC, P]), op=ALU.mult),
                    reads=[bank.b, par.b], writes=[hT2n.b])

        def upB(n, fps):
            hT2n = hT2[n % 2]
            for fp in fps:
                bank = PB.next()
                for hh in range(2):
                    fc = 2 * fp + hh
                    for kc in range(KC):
                        mm(bank, bank[:, hh * TB:(hh + 1) * TB], Wup[:, kc, fc * P:(fc + 1) * P], hT2n[:, kc, :],
                           [WupB[fc * P // 512], hT2n.b], kc == 0, kc == KC - 1)
                r = R512.next()
                S.op("act", lambda h, bank=bank, r=r: h.activation(out=r[:], in_=split2(bank), func=AF.Relu),
                     reads=[bank.b], writes=[r.b])
                eng = "dve" if fp % 2 == 0 else "pool"
                S.op(eng, lambda h, r=r, fp=fp: h.tensor_tensor(out=hid[:, 2 * fp:2 * fp + 2, :], in0=r[:], in1=r[:], op=ALU.mult),
                     reads=[r.b], writes=[hidB[fp]])

        def downB(n):
            t0 = n * TB
            xs = xinB[n % 3]
            S.op("dve", lambda h: h.memset(colsF[:, 0:NT], 0.0), writes=[colsF.b])
            for j in range(NT):
                for g in range(2):
                    bank = PB.next()
                    for fc in range(DFF // P):
                        mm(bank, bank[:], hid[:, fc, j * P:(j + 1) * P], Wdn[:, fc, g * 512:(g + 1) * 512], [hidB[fc // 2], WdnB[fc // 4]],
                           fc == 0, fc == DFF // P - 1)
                    S.op("dve", lambda h, bank=bank, j=j, g=g: h.tensor_tensor(out=xs[j][:, g * 512:(g + 1) * 512], in0=bank[:],
                                                                              in1=xs[j][:, g * 512:(g + 1) * 512], op=ALU.add),
                         reads=[bank.b, xs[j].b], writes=[xs[j].b])
                S.op("act", lambda h, j=j: h.activation(out=junkF[:], in_=xs[j][:], func=AF.Square, accum_out=colsF[:, j:j + 1]),
                     reads=[xs[j].b, colsF.b], writes=[junkF.b, colsF.b])
                S.op("dve", lambda h, j=j: h.tensor_scalar(out=colsF[:, 2 + j:3 + j], in0=colsF[:, j:j + 1], scalar1=1.0 / D, scalar2=pcol(C_EPSN), op0=ALU.mult, op1=ALU.add),
                     reads=[colsF.b, par.b], writes=[colsF.b])
                S.op("pool", lambda h, j=j: h.tensor_tensor(out=colsF[:, 4 + j:5 + j], in0=colsF[:, 2 + j:3 + j], in1=cexp[:, 1:2], op=ALU.pow),
                     reads=[colsF.b, cexp.b], writes=[colsF.b])
                S.op("dve", lambda h, j=j: h.scalar_tensor_tensor(out=xs[j][:], in0=xs[j][:], scalar=colsF[:, 4 + j:5 + j], in1=nfw[:],
                                                                   op0=ALU.mult, op1=ALU.mult),
                     reads=[xs[j].b, colsF.b, nfw.b], writes=[xs[j].b])
                S.dma("sp", lambda h, j=j: h.dma_start(out=y_d[t0 + j * P:t0 + (j + 1) * P, :], in_=xs[j][:]), reads=[xs[j].b], writes=[S.buf()])

        NFP = DFF // P // 2
        for n in range(min(2, NB)):
            stageB_a(n)
            stageB_b(n)
        for n in range(NB):
            upB(n, range(0, NFP // 2))
            if n + 2 < NB:
                stageB_a(n + 2)
            upB(n, range(NFP // 2, NFP))
            if n + 2 < NB:
                stageB_b(n + 2)
            downB(n)
        S.run()
    return nc


def host_consts(inp):
    par = np.zeros((P, NPAR), np.float32)
    dw = np.asarray(inp["dw_conv_w"], np.float32)[0, :, 0, :]
    for c in range(4):
        par[:, C_DW + c * KW:C_DW + (c + 1) * KW] = dw[:, c * P:(c + 1) * P].T
    col = lambda v, n: np.asarray(v, np.float32).reshape(n, P).T
    par[:, C_DWB:C_DWB + 4] = col(inp["dw_conv_b"][0], 4)
    par[:, C_LNW:C_LNW + 4] = col(inp["conv_ln_w"][0], 4)
    par[:, C_LNB:C_LNB + 4] = col(inp["conv_ln_b"][0], 4)
    par[:, C_BCO:C_BCO + 8] = col(inp["b_conv_out"][0], 8)
    par[:, C_LB0:C_LB0 + 4] = col(inp["hgrn_lb"][0], 4)
    par[:, C_LB1:C_LB1 + 4] = col(inp["hgrn_lb"][1], 4)
    par[:, C_HNW:C_HNW + 4] = col(inp["hgrn_norm_w"][0], 4)
    par[:, C_NW1:C_NW1 + 8] = col(inp["norm_mix_w"][0], 8)
    par[:, C_NW2:C_NW2 + 8] = col(inp["norm_mlp_w"][0], 8)
    par[:, C_EPSN] = 1e-6
    par[:, C_EPSL] = 1e-5
    nfw = np.ascontiguousarray(np.broadcast_to(np.asarray(inp["norm_final_w"], np.float32).reshape(1, D), (P, D)))
    ident = np.eye(P, dtype=np.float32).astype(ml_dtypes.bfloat16)
    ones2 = np.concatenate([np.full((P, P), 1.0 / 512, np.float32), np.full((P, P), 1.0 / 128, np.float32)], 1).astype(ml_dtypes.bfloat16)
    s = np.arange(P)[:, None]
    t = np.arange(P)[None, :]
    mask = ((s // CH == t // CH) & (s <= t)).astype(np.float32).astype(ml_dtypes.bfloat16)
    wd = np.zeros((4, 32, 16, 8, 32), np.float32)
    ci = np.arange(32)
    for q in range(16):
        for g in range(8):
            for j in range(4):
                if 4 * g + j < KW:
                    wd[j, ci, q, g, ci] = dw[4 * g + j, q * 32 + ci]
    wd = np.ascontiguousarray(wd.reshape(P, 16 * 8 * 32))
    return dict(params=par, nfw=nfw, ident=ident, ones2=ones2, mask=mask, wd=wd)


_NC_CACHE = {}


def run(inp, T, ncores):
    if T not in _NC_CACHE:
        _NC_CACHE[T] = build(T)
    nc = _NC_CACHE[T]
    c = host_consts(inp)
    shared = dict(
        w_in=np.ascontiguousarray(np.asarray(inp["w_in"], np.float32)[0]),
        w_conv_out=np.ascontiguousarray(np.asarray(inp["w_conv_out"], np.float32)[0]),
        w_hgrn_out=np.ascontiguousarray(np.asarray(inp["w_hgrn_out"], np.float32)[0]),
        w_out=np.ascontiguousarray(np.asarray(inp["w_out"], np.float32)[0]),
        w_mlp_up=np.ascontiguousarray(np.asarray(inp["w_mlp_up"], np.float32)[0]),
        w_mlp_down=np.ascontiguousarray(np.asarray(inp["w_mlp_down"], np.float32)[0]),
        **c)
    x = np.asarray(inp["x"], np.float32)
    in_maps = [dict(shared, x=np.ascontiguousarray(x[b])) for b in range(ncores)]
    res = run_bass_kernel_spmd(nc, in_maps, core_ids=list(range(ncores)))
    return np.stack([np.asarray(r["y"], np.float32) for r in res.results], 0)


def kernel(**inputs):
    return run(inputs, SEQ, NCORES)
```
